# Optimizing a Trainium2 kernel written in Bass

```python
import math
import jax
import jax.numpy as jnp
from jax import lax
import numpy as np

D_MODEL = 1024
BATCH = 2
SEQ = 8192
DEPTH = 1

HEAD_DIM = 64
NSA_HEADS = 8
NSA_KV_GROUPS = 2
NSA_HPG = NSA_HEADS // NSA_KV_GROUPS
RWKV_HEADS = 8
D_NSA = NSA_HEADS * HEAD_DIM
D_RWKV = RWKV_HEADS * HEAD_DIM
D_MIX = D_NSA + D_RWKV
D_KV = NSA_KV_GROUPS * HEAD_DIM
CMP_BLOCK = 32
CMP_STRIDE = 16
CMP_HIDDEN = 128
SEL_BLOCK = 64
SEL_TOPN = 16
WINDOW = 512
Q_BLOCK = 128
N_BUCKETS = 32
MAX_DISTANCE = 128
LORA_W = 64
LORA_A = 64
LORA_G = 128
D_FF = 2816
CONV_W = 3
NORM_EPS = 1e-6
GN_EPS = 64e-5
NEG_INF = -1e30
FORCE_SCORE = 1e9
NSA_SPLITS = (D_NSA, D_KV, D_KV, D_KV, D_KV, D_KV, D_KV, 3 * NSA_HEADS)
RWKV_SPLITS = (D_RWKV, D_RWKV, D_RWKV, LORA_W, LORA_A, LORA_G)
D_NSA_IN = sum(NSA_SPLITS)
D_RWKV_IN = sum(RWKV_SPLITS)
D_IN = D_NSA_IN + D_RWKV_IN

kernel_name = 'hymba_nsa_rwkv7_convffn'


def rmsnorm(x, g, eps=NORM_EPS):
    xf = x.astype(jnp.float32)
    y = xf * lax.rsqrt(jnp.mean(xf * xf, axis=-1, keepdims=True) + eps)
    return (y * g).astype(x.dtype)


def t5_bucket(dist):
    n = jnp.maximum(dist, 0)
    max_exact = N_BUCKETS // 2
    nf = jnp.maximum(n, 1).astype(jnp.float32)
    large = max_exact + (jnp.log(nf / max_exact) / math.log(MAX_DISTANCE / max_exact)
                         * (N_BUCKETS - max_exact)).astype(jnp.int32)
    large = jnp.minimum(large, N_BUCKETS - 1)
    return jnp.where(n < max_exact, n, large)


def token_shift(z):
    return jnp.pad(z, ((0, 0), (1, 0), (0, 0)))[:, :-1]


def causal_dwconv(z, w, b):
    T = z.shape[1]
    zp = jnp.pad(z, ((0, 0), (CONV_W - 1, 0), (0, 0)))
    return b + sum(zp[:, i:i + T] * w[i] for i in range(CONV_W))


def compress(kv, pos, w1, b1, w2, b2):
    B, T, G, dh = kv.shape
    nc = (T - CMP_BLOCK) // CMP_STRIDE + 1
    idx = jnp.arange(nc)[:, None] * CMP_STRIDE + jnp.arange(CMP_BLOCK)[None, :]
    blk = kv[:, idx] + pos[None, None, :, None, :]
    blk = blk.transpose(0, 1, 3, 2, 4).reshape(B, nc, G, CMP_BLOCK * dh)
    return jax.nn.gelu(blk @ w1 + b1) @ w2 + b2


def nsa_attention(q, k_cmp, v_cmp, k_sel, v_sel, k_win, v_win, gates, rel_bias):
    B, T = q.shape[0], q.shape[1]
    G, R, dh = NSA_KV_GROUPS, NSA_HPG, HEAD_DIM
    nc = k_cmp.shape[1]
    ns = T // SEL_BLOCK
    n_top = min(SEL_TOPN, ns)
    scale = dh ** -0.5
    bias_grq = rel_bias.reshape(N_BUCKETS, G, R)
    bias_gbr = bias_grq.transpose(1, 0, 2)
    cmp_start = jnp.arange(nc) * CMP_STRIDE
    cmp_end = cmp_start + CMP_BLOCK - 1
    sel_start = jnp.arange(ns) * SEL_BLOCK
    overlap = ((cmp_start[:, None] < sel_start[None, :] + SEL_BLOCK)
               & (cmp_end[:, None] >= sel_start[None, :])).astype(jnp.float32)
    ks_blk = k_sel.reshape(B, ns, SEL_BLOCK, G, dh).transpose(0, 3, 1, 2, 4)
    vs_blk = v_sel.reshape(B, ns, SEL_BLOCK, G, dh).transpose(0, 3, 1, 2, 4)
    kw_pad = jnp.pad(k_win, ((0, 0), (WINDOW, 0), (0, 0), (0, 0)))
    vw_pad = jnp.pad(v_win, ((0, 0), (WINDOW, 0), (0, 0), (0, 0)))
    b_ix = jnp.arange(B)[:, None, None, None]
    g_ix = jnp.arange(G)[None, :, None, None]
    blk_ix = jnp.arange(SEL_BLOCK)
    win_ix = jnp.arange(Q_BLOCK + WINDOW)
    sel_ix = jnp.arange(ns)[None, :]

    def query_block(qb):
        q0 = qb * Q_BLOCK
        t = q0 + jnp.arange(Q_BLOCK)
        qq = lax.dynamic_slice_in_dim(q, q0, Q_BLOCK, axis=1).reshape(B, Q_BLOCK, G, R, dh)
        gg = lax.dynamic_slice_in_dim(gates, q0, Q_BLOCK, axis=1).reshape(B, Q_BLOCK, G, R, 3)

        d_c = t[:, None] - cmp_end[None, :]
        m_c = d_c >= 0
        l_c = (jnp.einsum('bqgrd,bcgd->bgrqc', qq, k_cmp).astype(jnp.float32) * scale
               + bias_grq[t5_bucket(d_c)].transpose(2, 3, 0, 1))
        p_c = jax.nn.softmax(jnp.where(m_c, l_c, NEG_INF), axis=-1) * m_c
        o_c = jnp.einsum('bgrqc,bcgd->bqgrd', p_c.astype(v_cmp.dtype), v_cmp)

        imp = jnp.einsum('bgrqc,cn->bgqn', p_c, overlap)
        cur = (t // SEL_BLOCK)[:, None]
        forced = (sel_ix == 0) | (sel_ix == cur) | (sel_ix == cur - 1)
        score = jnp.where(sel_start[None, :] <= t[:, None],
                          jnp.where(forced, FORCE_SCORE, imp), NEG_INF)
        _, top_idx = lax.top_k(score, n_top)
        k_s = ks_blk[b_ix, g_ix, top_idx].reshape(B, G, Q_BLOCK, n_top * SEL_BLOCK, dh)
        v_s = vs_blk[b_ix, g_ix, top_idx].reshape(B, G, Q_BLOCK, n_top * SEL_BLOCK, dh)
        s_pos = (top_idx[..., None] * SEL_BLOCK + blk_ix).reshape(B, G, Q_BLOCK, n_top * SEL_BLOCK)
        d_s = t[None, None, :, None] - s_pos
        m_s = (d_s >= 0)[:, :, None]
        l_s = (jnp.einsum('bqgrd,bgqkd->bgrqk', qq, k_s).astype(jnp.float32) * scale
               + bias_gbr[g_ix, t5_bucket(d_s)].transpose(0, 1, 4, 2, 3))
        p_s = jax.nn.softmax(jnp.where(m_s, l_s, NEG_INF), axis=-1)
        o_s = jnp.einsum('bgrqk,bgqkd->bqgrd', p_s.astype(v_s.dtype), v_s)

        k_w = lax.dynamic_slice_in_dim(kw_pad, q0, Q_BLOCK + WINDOW, axis=1)
        v_w = lax.dynamic_slice_in_dim(vw_pad, q0, Q_BLOCK + WINDOW, axis=1)
        s_w = q0 - WINDOW + win_ix
        d_w = t[:, None] - s_w[None, :]
        m_w = (d_w >= 0) & (d_w < WINDOW) & (s_w[None, :] >= 0)
        l_w = (jnp.einsum('bqgrd,bkgd->bgrqk', qq, k_w).astype(jnp.float32) * scale
               + bias_grq[t5_bucket(d_w)].transpose(2, 3, 0, 1))
        p_w = jax.nn.softmax(jnp.where(m_w, l_w, NEG_INF), axis=-1)
        o_w = jnp.einsum('bgrqk,bkgd->bqgrd', p_w.astype(v_w.dtype), v_w)

        o = gg[..., 0:1] * o_c + gg[..., 1:2] * o_s + gg[..., 2:3] * o_w
        return o.reshape(B, Q_BLOCK, D_NSA)

    out = lax.map(query_block, jnp.arange(T // Q_BLOCK))
    return out.transpose(1, 0, 2, 3).reshape(B, T, D_NSA)


def _wkv7_step(state, inp):
    r, w, k, v, a, b = inp
    sa = jnp.einsum('bhvk,bhk->bhv', state, a)
    state = state * w[:, :, None, :] + sa[..., None] * b[:, :, None, :] + v[..., None] * k[:, :, None, :]
    return state, jnp.einsum('bhvk,bhk->bhv', state, r)


def rwkv7_time_mix(feats, w0, w2, a0, a2, g2, k_k, k_a, r_k, ln_w, ln_b):
    B, T, _ = feats.shape
    H, N = RWKV_HEADS, HEAD_DIM
    r, k, v, xw, xa, xg = jnp.split(feats, np.cumsum(RWKV_SPLITS)[:-1].tolist(), axis=-1)
    w = -jax.nn.softplus(-(w0 + jnp.tanh(xw) @ w2)) - 0.5
    decay = jnp.exp(-jnp.exp(w.astype(jnp.float32)))
    a = jax.nn.sigmoid(a0 + xa @ a2)
    g = jax.nn.sigmoid(xg) @ g2

    def heads(z):
        return z.reshape(B, T, H, N).astype(jnp.float32)

    kk = heads(k * k_k)
    kk = kk / jnp.maximum(jnp.sqrt(jnp.sum(kk * kk, axis=-1, keepdims=True)), 1e-12)
    k = k * (1.0 + (a - 1.0) * k_a)
    rh, kh, vh, ah = heads(r), heads(k), heads(v), heads(a)
    xs = tuple(z.transpose(1, 0, 2, 3) for z in (rh, heads(decay), kh, vh, -kk, kk * ah))
    state0 = jnp.zeros((B, H, N, N), jnp.float32)
    _, y = lax.scan(_wkv7_step, state0, xs)
    y = y.transpose(1, 0, 2, 3)
    mu = jnp.mean(y, axis=-1, keepdims=True)
    var = jnp.mean(jnp.square(y - mu), axis=-1, keepdims=True)
    y = ((y - mu) * lax.rsqrt(var + GN_EPS)).reshape(B, T, D_RWKV) * ln_w + ln_b
    bonus = jnp.sum(rh * kh * r_k, axis=-1, keepdims=True) * vh
    y = (y + bonus.reshape(B, T, D_RWKV)) * g
    return y.astype(feats.dtype)


def setup_inputs(seed: int = 0) -> dict:
    key = jax.random.key(seed)
    ks = list(jax.random.split(key, 32))
    L = DEPTH

    def nrm(shape, scale):
        return scale * jax.random.normal(ks.pop(), shape, jnp.float32)

    def unif(shape, lo, hi):
        return jax.random.uniform(ks.pop(), shape, jnp.float32, minval=lo, maxval=hi)

    return {
        'x': nrm((BATCH, SEQ, D_MODEL), 1.0),
        'norm1_g': 1.0 + nrm((L, D_MODEL), 0.02),
        'w_in': nrm((L, D_MODEL, D_IN), D_MODEL ** -0.5),
        'q_norm_g': 1.0 + nrm((L, HEAD_DIM), 0.02),
        'k_norm_g': 1.0 + nrm((L, 3, HEAD_DIM), 0.02),
        'cmp_pos': nrm((L, 2, CMP_BLOCK, HEAD_DIM), 0.02),
        'cmp_w1': nrm((L, 2, CMP_BLOCK * HEAD_DIM, CMP_HIDDEN), (CMP_BLOCK * HEAD_DIM) ** -0.5),
        'cmp_b1': nrm((L, 2, CMP_HIDDEN), 0.02),
        'cmp_w2': nrm((L, 2, CMP_HIDDEN, HEAD_DIM), CMP_HIDDEN ** -0.5),
        'cmp_b2': nrm((L, 2, HEAD_DIM), 0.02),
        'rel_bias': nrm((N_BUCKETS, NSA_HEADS), 0.5),
        'rwkv_mu': unif((L, D_RWKV_IN), 0.0, 1.0),
        'w0': unif((L, D_RWKV), -6.0, -1.0),
        'w2': nrm((L, LORA_W, D_RWKV), 0.1 * LORA_W ** -0.5),
        'a0': nrm((L, D_RWKV), 0.1),
        'a2': nrm((L, LORA_A, D_RWKV), 0.5 * LORA_A ** -0.5),
        'g2': nrm((L, LORA_G, D_RWKV), LORA_G ** -0.5),
        'k_k': 0.85 + nrm((L, D_RWKV), 0.02),
        'k_a': 1.0 + nrm((L, D_RWKV), 0.02),
        'r_k': nrm((L, RWKV_HEADS, HEAD_DIM), 0.1),
        'ln_x_w': 1.0 + nrm((L, D_RWKV), 0.02),
        'ln_x_b': nrm((L, D_RWKV), 0.02),
        'w_out': nrm((L, D_MIX, D_MODEL), D_MIX ** -0.5),
        'norm2_g': 1.0 + nrm((L, D_MODEL), 0.02),
        'ffn_up': nrm((L, D_MODEL, 2 * D_FF), D_MODEL ** -0.5),
        'conv_w': nrm((L, CONV_W, 2 * D_FF), CONV_W ** -0.5),
        'conv_b': nrm((L, 2 * D_FF), 0.02),
        'ffn_down': nrm((L, D_FF, D_MODEL), D_FF ** -0.5),
    }


def reference(x, norm1_g, w_in, q_norm_g, k_norm_g, cmp_pos, cmp_w1, cmp_b1, cmp_w2, cmp_b2,
              rel_bias, rwkv_mu, w0, w2, a0, a2, g2, k_k, k_a, r_k, ln_x_w, ln_x_b,
              w_out, norm2_g, ffn_up, conv_w, conv_b, ffn_down):
    B, T, _ = x.shape
    G, H, dh = NSA_KV_GROUPS, NSA_HEADS, HEAD_DIM
    nsa_cuts = np.cumsum(NSA_SPLITS)[:-1].tolist()

    def kv_heads(z):
        return z.reshape(B, T, G, dh)

    for l in range(DEPTH):
        h = rmsnorm(x, norm1_g[l])
        proj = h @ w_in[l]
        q, kc, vc, ksl, vsl, kwn, vwn, gl = jnp.split(proj[..., :D_NSA_IN], nsa_cuts, axis=-1)
        q = rmsnorm(q.reshape(B, T, H, dh), q_norm_g[l])
        k_cmp = rmsnorm(compress(kv_heads(kc), cmp_pos[l, 0], cmp_w1[l, 0], cmp_b1[l, 0],
                                 cmp_w2[l, 0], cmp_b2[l, 0]), k_norm_g[l, 0])
        v_cmp = compress(kv_heads(vc), cmp_pos[l, 1], cmp_w1[l, 1], cmp_b1[l, 1],
                         cmp_w2[l, 1], cmp_b2[l, 1])
        k_sel = rmsnorm(kv_heads(ksl), k_norm_g[l, 1])
        k_win = rmsnorm(kv_heads(kwn), k_norm_g[l, 2])
        gates = jax.nn.sigmoid(gl).reshape(B, T, H, 3)
        o_nsa = nsa_attention(q, k_cmp, v_cmp, k_sel, kv_heads(vsl), k_win, kv_heads(vwn),
                              gates, rel_bias)

        rw = proj[..., D_NSA_IN:]
        rw = rw + (token_shift(rw) - rw) * rwkv_mu[l]
        o_rwkv = rwkv7_time_mix(rw, w0[l], w2[l], a0[l], a2[l], g2[l], k_k[l], k_a[l],
                                r_k[l], ln_x_w[l], ln_x_b[l])
        x = x + jnp.concatenate([o_nsa, o_rwkv], axis=-1) @ w_out[l]

        h = rmsnorm(x, norm2_g[l])
        u = causal_dwconv(h @ ffn_up[l], conv_w[l], conv_b[l])
        u_val, u_gate = jnp.split(u, 2, axis=-1)
        x = x + (jax.nn.silu(u_gate) * u_val) @ ffn_down[l]
    return x
```

```python
import math
from contextlib import ExitStack

import numpy as np
import concourse.bass as bass
import concourse.mybir as mybir
from concourse.bass_utils import run_bass_kernel_spmd

F32 = mybir.dt.float32
BF16 = mybir.dt.bfloat16
AF = mybir.ActivationFunctionType
ALU = mybir.AluOpType
AX = mybir.AxisListType

NT = 8192
OWN0 = 6144
NOWN = 2048
C0 = math.exp(-0.5)
import os
P3LVL = int(os.environ.get('P3LVL', '9'))
P3SUB = int(os.environ.get('P3SUB', '9'))


class Buf:
    __slots__ = ("t", "w", "rs", "rd", "name", "psum")

    def __init__(self, t, name=""):
        self.t = t
        self.psum = False
        self.w = None
        self.rs = {}
        self.rd = []
        self.name = name

    def __getitem__(self, k):
        return self.t[k]


class Eng:
    def __init__(self, name, e, sem):
        self.name = name
        self.e = e
        self.sem = sem
        self.n = 0
        self.seen = {}


class DmaTok:
    def __init__(self, sem, val):
        self.sem = sem
        self.val = val


class KB:
    def __init__(self, nc, stack):
        self.nc = nc
        self.stack = stack
        self.engs = {}
        for name, e in (("pe", nc.tensor), ("act", nc.scalar), ("dve", nc.vector),
                        ("pool", nc.gpsimd), ("sp", nc.sync)):
            sem = stack.enter_context(nc.semaphore("s_" + name))
            self.engs[name] = Eng(name, e, sem)
        self.ND = 32
        self.dsem = [stack.enter_context(nc.semaphore("d%d" % i)) for i in range(self.ND)]
        self.dcount = [0] * self.ND
        self.di = 0
        self.nbuf = 0

    def sb(self, shape, dt, st=None):
        self.nbuf += 1
        name = "b%d" % self.nbuf
        return Buf((st or self.stack).enter_context(self.nc.sbuf_tensor(name, shape, dt)), name)

    def ps(self, shape, dt=F32):
        self.nbuf += 1
        name = "p%d" % self.nbuf
        b = Buf(self.stack.enter_context(self.nc.psum_tensor(name, shape, dt)), name)
        b.psum = True
        return b

    def _wait(self, X, dep):
        if dep is None:
            return
        src, n = dep
        if isinstance(src, DmaTok):
            key = (id(src.sem), src.val)
            if X.seen.get(key):
                return
            X.e.wait_ge(src.sem, src.val)
            X.seen[key] = 1
            return
        if src is X and X.name == "pe":
            return
        if X.seen.get(src.name, 0) >= n:
            return
        X.e.wait_ge(src.sem, n)
        X.seen[src.name] = n

    def _deps(self, X, reads, writes):
        for b in reads:
            self._wait(X, b.w)
            if b.psum:
                for name, n in b.rs.items():
                    if name != X.name:
                        self._wait(X, (self.engs[name], n))
        for b in writes:
            self._wait(X, b.w)
            for name, n in b.rs.items():
                self._wait(X, (self.engs[name], n))
            for t in b.rd:
                self._wait(X, (t, 0))

    def _mark(self, tok, reads, writes):
        for b in reads:
            if isinstance(tok[0], DmaTok):
                b.rd.append(tok[0])
            else:
                b.rs[tok[0].name] = tok[1]
        for b in writes:
            b.w = tok
            b.rs = {}
            b.rd = []

    def op(self, eng, fn, reads=(), writes=()):
        X = self.engs[eng]
        self._deps(X, reads, writes)
        ins = fn(X.e)
        X.n += 1
        ins.then_inc(X.sem, 1)
        self._mark((X, X.n), reads, writes)
        return ins

    def dma(self, out, in_, reads=(), writes=(), eng="sp"):
        X = self.engs[eng]
        i = self.di
        self.di = (self.di + 1) % self.ND
        if self.dcount[i] > 0:
            key = (id(self.dsem[i]), 16 * self.dcount[i])
            if not X.seen.get(key):
                X.e.wait_ge(self.dsem[i], 16 * self.dcount[i])
                X.seen[key] = 1
        self._deps(X, reads, writes)
        self.dcount[i] += 1
        tok = DmaTok(self.dsem[i], 16 * self.dcount[i])
        X.e.dma_start(out=out, in_=in_).then_inc(self.dsem[i], 16)
        self._mark((tok, 0), reads, writes)
        return tok

    def barrier(self):
        for X in self.engs.values():
            for i in range(self.ND):
                if self.dcount[i] > 0:
                    key = (id(self.dsem[i]), 16 * self.dcount[i])
                    if not X.seen.get(key):
                        X.e.wait_ge(self.dsem[i], 16 * self.dcount[i])
                        X.seen[key] = 1
            for name, E in self.engs.items():
                if E.n > 0 and X.seen.get(name, 0) < E.n:
                    X.e.wait_ge(E.sem, E.n)
                    X.seen[name] = E.n

    def finish(self):
        X = self.engs["sp"]
        for i in range(self.ND):
            if self.dcount[i] > 0:
                X.e.wait_ge(self.dsem[i], 16 * self.dcount[i])
        for name, E in self.engs.items():
            if name != "sp" and E.n > 0:
                X.e.wait_ge(E.sem, E.n)


def build(dbg=False, ntiles=64, phases=(1, 2, 3, 4), p4tiles=17, p3tiles=17):
    nc = bass.Bass("TRN2", target_bir_lowering=False)

    def din(name, shape, dt=F32):
        return nc.dram_tensor(name, shape, dt, kind="ExternalInput").ap()

    def dscr(name, shape, dt):
        return nc.dram_tensor(name, shape, dt, kind="Internal").ap()

    def dout(name, shape, dt=F32):
        return nc.dram_tensor(name, shape, dt, kind="ExternalOutput").ap()

    xp = din("xp", [NT, 1024])
    g1 = din("g1", [1, 1024])
    w_rw = din("w_rw", [1024, 1792])
    mu = din("mu", [1, 1792])
    w_tm = din("w_tm", [1024, 768])
    rwp = din("rwp", [7, 512])
    w2 = din("w2", [64, 512])
    a2 = din("a2", [64, 512])
    g2 = din("g2", [128, 512])
    kngr = din("kngr", [1, 256])
    tri = din("tri", [3, 128, 128])
    idf_d = din("idf", [128, 128])
    bo_d = din("bo", [128, 128])

    w_out = din("w_out", [1024, 1024])
    g2n = din("g2n", [1, 1024])
    ffn_up = din("ffn_up", [1024, 5632])
    ffn_down = din("ffn_down", [2816, 1024])
    cw = din("cw", [128, 44, 3])
    cb = din("cb", [128, 44])
    hmask = din("hmask", [128, 1])
    w_q = din("w_q", [1024, 512])
    w_g = din("w_g", [1024, 24])
    qng = din("qng", [128, 1])
    kcg = din("kcg", [128, 1])
    cw1 = din("cw1", [2, 128, 32, 128])
    cpos = din("cpos", [2, 128, 32])
    cb1 = din("cb1", [128, 2])
    cw2 = din("cw2", [2, 128, 64])
    cb2k = din("cb2k", [128, 1])
    cb2v = din("cb2v", [1, 64])
    bmc = din("bmc", [18, 2, 128, 512])
    b31t = din("b31t", [2, 128, 512])
    bt_diag = din("bt_diag", [2, 128, 512])
    bt_sub = din("bt_sub", [2, 128, 512])
    ovl = din("ovl", [128, 4, 132])
    bige = din("bige", [128, NT], BF16)
    gsel = din("gsel", [24, 24 * 64], BF16)
    selkeep = din("selkeep", [17, 128, 128])
    seladd = din("seladd", [17, 128, 128])
    selv = din("selv", [17, 128, 128])
    cvoid = din("cvoid", [128, 4])
    onesv = din("onesv", [128, 64, 64], BF16)
    KS2 = [dscr("ks2_%d" % g, [128, NT], BF16) for g in range(2)]
    KW2 = [dscr("kw2_%d" % g, [128, NT], BF16) for g in range(2)]
    KCT = dscr("kct", [128, NT], BF16)
    VCT = dscr("vct", [128, NT], BF16)
    VSW = dscr("vsw", [NT, 256], BF16)
    ORT = dscr("ort", [512, NOWN + 128], BF16)
    HTO = dscr("hto", [128, 8, NOWN + 128], BF16)

    if 3 in phases or 4 not in phases:
        ONT = dscr("ont", [512, NOWN + 128], BF16)
    else:
        ONT = din("ont", [512, NOWN + 128], BF16)
        ORT = din("ort_in", [512, NOWN + 128], BF16)
    X1S = dscr("x1s", [NOWN + 128, 1024], F32)
    H2S = dscr("h2s", [128, 8, NOWN + 128], BF16)
    y = dout("y", [NOWN, 1024])
    dbg_o = {}
    if dbg:
        dbg_o["orw"] = dout("d_orw", [NOWN + 128, 512])
        dbg_o["ks2"] = dout("d_ks2", [128, NT], BF16)
        dbg_o["vsw"] = dout("d_vsw", [NT, 256], BF16)
        dbg_o["ont"] = dout("d_ont", [512, NOWN + 128], BF16)
        dbg_o["kc"] = dout("d_kc", [2, 128, 512])
        dbg_o["vc"] = dout("d_vc", [2, 128, 4, 64])
        dbg_o["imp"] = dout("d_imp", [17, 2, 128, 128])

    with ExitStack() as st:
        kb = KB(nc, st)
        op = kb.op
        PT = kb.ps([128, 1024], BF16)
        banks = [kb.ps([128, 512], F32) for _ in range(7)]
        bank_i = [0]

        def bank():
            b = banks[bank_i[0] % 7]
            bank_i[0] += 1
            return b

        def mm(ob, oap, lb, lap, rb, rap, start=True, stop=True):
            op("pe", lambda e: e.matmul(oap, lhsT=lap, rhs=rap, start=start, stop=stop),
               reads=[lb, rb], writes=[ob])

        idf = kb.sb([128, 128], F32)
        idb = kb.sb([128, 128], BF16)
        TRI = [kb.sb([128, 128], F32) for _ in range(3)]
        kb.dma(idf[:], idf_d, writes=[idf])
        op("dve", lambda e: e.tensor_copy(out=idb[:], in_=idf[:]), reads=[idf], writes=[idb])
        for i in range(3):
            kb.dma(TRI[i][:], tri[i], writes=[TRI[i]])
        UI, US, LS = TRI
        BO = kb.sb([128, 128], F32)
        kb.dma(BO[:], bo_d, writes=[BO])
        ones_c = kb.sb([128, 1], F32)
        op("pool", lambda e: e.memset(ones_c[:], 1.0), writes=[ones_c])

        if 1 in phases:
            phase1(nc, kb, st, locals())
        if 3 in phases:
            kb.barrier()
            phase3(nc, kb, st, locals())
        if 4 in phases:
            kb.barrier()
            phase4(nc, kb, st, locals())
        kb.finish()
    return nc


def phase1(nc, kb, st0, G):
    op = kb.op
    mm = G["mm"]
    bank = G["bank"]
    PT = G["PT"]
    idf, idb, UI, US, LS, ones_c = G["idf"], G["idb"], G["UI"], G["US"], G["LS"], G["ones_c"]
    xp, g1, w_rw, mu, w_tm, rwp = G["xp"], G["g1"], G["w_rw"], G["mu"], G["w_tm"], G["rwp"]
    w2, a2, g2, kngr = G["w2"], G["a2"], G["g2"], G["kngr"]
    KS2, KW2, KCT, VCT, VSW, ORT, HTO = G["KS2"], G["KW2"], G["KCT"], G["VCT"], G["VSW"], G["ORT"], G["HTO"]
    dbg_o = G["dbg_o"]
    ntiles = G["ntiles"]
    with ExitStack() as st:
        def sb(shape, dt):
            return kb.sb(shape, dt, st)

        WR1 = sb([128, 8, 1792], BF16)
        WR2 = sb([128, 8, 1792], BF16)
        WTM = sb([128, 8, 768], BF16)
        G1B = sb([128, 1024], F32)
        PRM = [sb([128, 512], F32) for _ in range(7)]
        W0B, A0B, KKB, KAB, RKB, LNW, LNB = PRM
        w2b = sb([64, 512], BF16)
        a2b = sb([64, 512], BF16)
        g2b = sb([128, 512], BF16)
        KNG = sb([128, 256], F32)
        kb.dma(G1B[:], g1.partition_broadcast(128), writes=[G1B])
        for i in range(7):
            kb.dma(PRM[i][:], rwp[i:i + 1, :].partition_broadcast(128), writes=[PRM[i]])
        kb.dma(KNG[:], kngr.partition_broadcast(128), writes=[KNG])
        with ExitStack() as st_setup:
            def sbs(shape, dt):
                return kb.sb(shape, dt, st_setup)
            mub = sbs([128, 1792], F32)
            omm = sbs([128, 1792], F32)
            kb.dma(mub[:], mu.partition_broadcast(128), writes=[mub])
            op("dve", lambda e: e.tensor_scalar(out=omm[:], in0=mub[:], scalar1=-1.0, scalar2=1.0,
                                                op0=ALU.mult, op1=ALU.add), reads=[mub], writes=[omm])
            stg = [sbs([128, 1792], F32) for _ in range(2)]
            stg2 = [sbs([128, 1024], F32) for _ in range(2)]
            for kc in range(8):
                s = stg[kc % 2]
                kb.dma(s[:], w_rw[kc * 128:(kc + 1) * 128, :], writes=[s])
                op("dve", lambda e: e.tensor_tensor(out=WR1[:, kc, :], in0=s[:], in1=omm[:], op=ALU.mult),
                   reads=[s, omm], writes=[WR1])
                op("pool", lambda e: e.tensor_tensor(out=WR2[:, kc, :], in0=s[:], in1=mub[:], op=ALU.mult),
                   reads=[s, mub], writes=[WR2])
                s2 = stg2[kc % 2]
                kb.dma(s2[:, 0:768], w_tm[kc * 128:(kc + 1) * 128, :], writes=[s2])
                op("act", lambda e: e.activation(out=WTM[:, kc, :], in_=s2[:, 0:768], func=AF.Copy),
                   reads=[s2], writes=[WTM])
            for src, dst, p in ((w2, w2b, 64), (a2, a2b, 64), (g2, g2b, 128)):
                s = stg[0]
                kb.dma(s[0:p, 0:512], src, writes=[s])
                op("dve", lambda e: e.tensor_copy(out=dst[:], in_=s[0:p, 0:512]), reads=[s], writes=[dst])
            kb.barrier()

        xt = [sb([128, 1024], F32) for _ in range(1)] * 2
        ss1 = sb([128, 1], F32)
        xn = sb([128, 1024], BF16)
        HT = [sb([128, 8, 129], BF16) for _ in range(2)]
        op("pool", lambda e: e.memset(HT[0][:, :, 0:1], 0.0), writes=[HT[0]])
        op("pool", lambda e: e.memset(HT[1][:, :, 0:1], 0.0), writes=[HT[1]])

        def f32t():
            return sb([128, 512], F32)

        Rs, Ks, Vs = f32t(), f32t(), f32t()
        Vb = [sb([128, 512], BF16) for _ in range(2)]
        thx = sb([64, 256], BF16)
        L16 = sb([128, 256], BF16)
        knT = sb([64, 512], BF16)
        kcT = sb([128, 256], BF16)
        k4 = sb([128, 4], F32)
        sgx = sb([128, 128], BF16)
        sg, av = f32t(), f32t()
        z1, za1 = sg, av
        kk, kkn, kp, bq = f32t(), f32t(), f32t(), f32t()
        tA = f32t()
        kk2 = t1 = tA
        s8 = sb([128, 8], F32)
        rn8 = sb([128, 8], F32)
        eL, eNL, eLex, eD = (sb([128, 512], BF16) for _ in range(4))
        rt = sb([128, 512], BF16)
        at = sb([128, 512], BF16)
        bt = sb([128, 512], BF16)
        kt = sb([128, 512], BF16)
        Bh = [sb([128, 512], BF16) for _ in range(2)]
        Kh = [sb([128, 512], BF16) for _ in range(2)]
        WC = [sb([64, 8], F32) for _ in range(2)]
        rtT = [sb([64, 1024], BF16) for _ in range(2)]
        atT = [sb([64, 1024], BF16) for _ in range(2)]
        btT = sb([64, 1024], BF16)
        ktT = sb([64, 1024], BF16)
        Nm = [sb([128, 512], BF16) for _ in range(2)]
        Mm = [sb([128, 512], BF16) for _ in range(2)]
        N2 = [sb([128, 512], BF16) for _ in range(2)]
        M2 = [sb([128, 512], BF16) for _ in range(2)]
        AKt = [[sb([128, 512], BF16) for _ in range(2)] for _ in range(2)]
        RBt = [[sb([128, 512], BF16) for _ in range(2)] for _ in range(2)]
        RKt = [[sb([128, 512], BF16) for _ in range(2)] for _ in range(2)]
        Pm = [[sb([128, 512], BF16) for _ in range(2)] for _ in range(2)]
        rk = tA
        rs8 = [sb([128, 8], F32) for _ in range(2)]
        Vkeep = [f32t() for _ in range(2)]
        Gkeep = [f32t() for _ in range(2)]
        ST = sb([64, 512], F32)
        STb = sb([64, 512], BF16)
        op("pool", lambda e: e.memset(ST[:], 0.0), writes=[ST])
        op("pool", lambda e: e.memset(STb[:], 0.0), writes=[STb])
        Gb = sb([128, 512], BF16)
        Ub = sb([128, 512], BF16)
        stmp = sb([64, 512], F32)
        ysb, ysq, yn = f32t(), f32t(), f32t()
        m8, v8, r8 = sb([128, 8], F32), sb([128, 8], F32), sb([128, 8], F32)
        ob16 = sb([128, 512], BF16)
        oT = sb([128, 512], BF16)
        ksq = tA
        krs = f32t()
        kn16 = sb([128, 512], BF16)
        kc16 = sb([128, 256], BF16)
        vsw16 = sb([128, 256], BF16)

        def b3(ap):
            return ap.rearrange("p (h d) -> p h d", h=8)

        def bc8(ap8, p=128):
            return ap8.unsqueeze(2).to_broadcast([p, 8, 64])

        def prep(tt, sgen=None):
            par = tt % 2
            own = tt >= OWN0 // 128 - 1
            x_t = xt[par]
            ht = HT[par]
            kb.dma(x_t[:], xp[tt * 128:(tt + 1) * 128, :], writes=[x_t])
            op("act", lambda e: e.activation(out=xn[:], in_=x_t[:], func=AF.Square, accum_out=ss1[:]),
               reads=[x_t], writes=[xn, ss1])
            op("dve", lambda e: e.tensor_scalar(out=ss1[:], in0=ss1[:], scalar1=1.0 / 1024, scalar2=1e-6,
                                                op0=ALU.mult, op1=ALU.add), reads=[ss1], writes=[ss1])
            op("act", lambda e: e.activation(out=ss1[:], in_=ss1[:], func=AF.Sqrt), reads=[ss1], writes=[ss1])
            op("dve", lambda e: e.reciprocal(out=ss1[:], in_=ss1[:]), reads=[ss1], writes=[ss1])
            op("dve", lambda e: e.scalar_tensor_tensor(out=xn[:], in0=x_t[:], scalar=ss1[:, 0:1], in1=G1B[:],
                                                       op0=ALU.mult, op1=ALU.mult),
               reads=[x_t, ss1, G1B], writes=[xn])
            for kc in range(8):
                op("pe", lambda e: e.transpose(out=PT[:, kc * 128:(kc + 1) * 128],
                                               in_=xn[:, kc * 128:(kc + 1) * 128], identity=idb[:]),
                   reads=[xn, idb], writes=[PT])
            op("act", lambda e: e.activation(out=ht[:, :, 1:129], in_=PT[:].rearrange("p (k t) -> p k t", k=8),
                                             func=AF.Copy), reads=[PT], writes=[ht])
            op("pool", lambda e: e.tensor_copy(out=HT[1 - par][:, :, 0:1], in_=ht[:, :, 128:129]),
               reads=[ht], writes=[HT[1 - par]])
            if own:
                kb.dma(HTO[:, :, (tt - 47) * 128:(tt - 46) * 128], ht[:, :, 1:129], reads=[ht])

            def tm_proj(c0, dst, eng):
                pb = bank()
                for j in range(16):
                    kc = j % 8
                    if j < 8:
                        mm(pb, pb[:], ht, ht[:, kc, 1:129], WR1, WR1[:, kc, c0:c0 + 512], start=(j == 0), stop=False)
                    else:
                        mm(pb, pb[:], ht, ht[:, kc, 0:128], WR2, WR2[:, kc, c0:c0 + 512], start=False, stop=(j == 15))
                if eng == "act":
                    op("act", lambda e: e.activation(out=dst[:], in_=pb[:], func=AF.Copy), reads=[pb], writes=[dst])
                else:
                    op("dve", lambda e: e.tensor_copy(out=dst[:], in_=pb[:]), reads=[pb], writes=[dst])
            tm_proj(0, Rs, "act")
            tm_proj(512, Ks, "dve")
            tm_proj(1024, Vs, "act")
            op("pool", lambda e: e.tensor_copy(out=Vb[par][:], in_=Vs[:]), reads=[Vs], writes=[Vb[par]])
            pA = bank()
            for kc in range(8):
                mm(pA, pA[:], ht, ht[:, kc, 1:129], WTM, WTM[:, kc, 0:512], start=(kc == 0), stop=(kc == 7))
            pB = bank()
            for kc in range(8):
                mm(pB, pB[:, 0:256], ht, ht[:, kc, 1:129], WTM, WTM[:, kc, 512:768], start=(kc == 0), stop=(kc == 7))
            pC = bank()
            for j in range(16):
                kc = j % 8
                if j < 8:
                    mm(pC, pC[:, 0:256], ht, ht[:, kc, 1:129], WR1, WR1[:, kc, 1536:1792], start=(j == 0), stop=False)
                else:
                    mm(pC, pC[:, 0:256], ht, ht[:, kc, 0:128], WR2, WR2[:, kc, 1536:1792], start=False, stop=(j == 15))
            op("act", lambda e: e.activation(out=vsw16[:], in_=pA[:, 0:256], func=AF.Copy), reads=[pA], writes=[vsw16])
            kb.dma(VSW[tt * 128:(tt + 1) * 128, :], vsw16[:], reads=[vsw16])
            op("act", lambda e: e.activation(out=ksq[:, 0:256], in_=pA[:, 256:512], func=AF.Square), reads=[pA], writes=[ksq])
            op("dve", lambda e: e.tensor_reduce(out=k4[:], in_=ksq[:, 0:256].rearrange("p (h d) -> p h d", h=4), axis=AX.X, op=ALU.add),
               reads=[ksq], writes=[k4])
            op("dve", lambda e: e.tensor_scalar(out=k4[:], in0=k4[:], scalar1=1.0 / 64, scalar2=1e-6, op0=ALU.mult, op1=ALU.add), reads=[k4], writes=[k4])
            op("act", lambda e: e.activation(out=k4[:], in_=k4[:], func=AF.Sqrt), reads=[k4], writes=[k4])
            op("dve", lambda e: e.reciprocal(out=k4[:], in_=k4[:]), reads=[k4], writes=[k4])
            op("dve", lambda e: e.tensor_tensor(out=krs[:, 0:256].rearrange("p (h d) -> p h d", h=4),
                                                in0=pA[:, 256:512].rearrange("p (h d) -> p h d", h=4),
                                                in1=k4[:].unsqueeze(2).to_broadcast([128, 4, 64]), op=ALU.mult), reads=[pA, k4], writes=[krs])
            op("pool", lambda e: e.tensor_tensor(out=kn16[:, 0:256], in0=krs[:, 0:256], in1=KNG[:], op=ALU.mult), reads=[krs, KNG], writes=[kn16])
            op("act", lambda e: e.activation(out=kc16[:], in_=pB[:, 0:256], func=AF.Copy), reads=[pB], writes=[kc16])
            op("act", lambda e: e.activation(out=L16[:, 0:64], in_=pC[:, 0:64], func=AF.Tanh), reads=[pC], writes=[L16])
            op("act", lambda e: e.activation(out=L16[:, 64:128], in_=pC[:, 64:128], func=AF.Copy), reads=[pC], writes=[L16])
            op("act", lambda e: e.activation(out=L16[:, 128:256], in_=pC[:, 128:256], func=AF.Tanh, scale=0.5), reads=[pC], writes=[L16])
            op("pool", lambda e: e.tensor_scalar(out=L16[:, 128:256], in0=L16[:, 128:256], scalar1=0.5, scalar2=0.5, op0=ALU.mult, op1=ALU.add),
               reads=[L16], writes=[L16])
            for i in range(4):
                op("pe", lambda e: e.transpose(out=PT[0:64, i * 128:(i + 1) * 128], in_=kn16[:, i * 64:(i + 1) * 64], identity=idb[:]),
                   reads=[kn16, idb], writes=[PT])
            for i in range(2):
                op("pe", lambda e: e.transpose(out=PT[:, 512 + i * 128:512 + (i + 1) * 128], in_=kc16[:, i * 128:(i + 1) * 128], identity=idb[:]),
                   reads=[kc16, idb], writes=[PT])
            for i in range(2):
                op("pe", lambda e: e.transpose(out=PT[0:64, 768 + i * 128:768 + (i + 1) * 128], in_=L16[:, i * 64:(i + 1) * 64], identity=idb[:]),
                   reads=[L16, idb], writes=[PT])
            op("act", lambda e: e.activation(out=knT[:], in_=PT[0:64, 0:512], func=AF.Copy), reads=[PT], writes=[knT])
            op("dve", lambda e: e.tensor_copy(out=kcT[:], in_=PT[:, 512:768]), reads=[PT], writes=[kcT])
            op("act", lambda e: e.activation(out=thx[:], in_=PT[0:64, 768:1024], func=AF.Copy), reads=[PT], writes=[thx])
            op("pe", lambda e: e.transpose(out=PT[:, 0:128], in_=L16[:, 128:256], identity=idb[:]), reads=[L16, idb], writes=[PT])
            op("dve", lambda e: e.tensor_copy(out=sgx[:], in_=PT[:, 0:128]), reads=[PT], writes=[sgx])
            for g in range(2):
                kb.dma(KS2[g][0:64, tt * 128:(tt + 1) * 128], knT[:, g * 128:(g + 1) * 128], reads=[knT])
                kb.dma(KW2[g][0:64, tt * 128:(tt + 1) * 128], knT[:, 256 + g * 128:256 + (g + 1) * 128], reads=[knT])
            kb.dma(KCT[:, tt * 128:(tt + 1) * 128], kcT[:, 0:128], reads=[kcT])
            kb.dma(VCT[:, tt * 128:(tt + 1) * 128], kcT[:, 128:256], reads=[kcT])

            pz = bank()
            mm(pz, pz[:], thx, thx[:, 0:128], w2b, w2b[:])
            op("dve", lambda e: e.tensor_tensor(out=z1[:], in0=pz[:], in1=W0B[:], op=ALU.add), reads=[pz, W0B], writes=[z1])
            op("act", lambda e: e.activation(out=sg[:], in_=z1[:], func=AF.Tanh, scale=0.5), reads=[z1], writes=[sg])
            op("pool", lambda e: e.tensor_scalar(out=sg[:], in0=sg[:], scalar1=0.5, scalar2=0.5, op0=ALU.mult, op1=ALU.add),
               reads=[sg], writes=[sg])
            pa = bank()
            mm(pa, pa[:], thx, thx[:, 128:256], a2b, a2b[:])
            op("dve", lambda e: e.tensor_tensor(out=za1[:], in0=pa[:], in1=A0B[:], op=ALU.add), reads=[pa, A0B], writes=[za1])
            op("act", lambda e: e.activation(out=av[:], in_=za1[:], func=AF.Tanh, scale=0.5), reads=[za1], writes=[av])
            op("pool", lambda e: e.tensor_scalar(out=av[:], in0=av[:], scalar1=0.5, scalar2=0.5, op0=ALU.mult, op1=ALU.add),
               reads=[av], writes=[av])
            if own:
                pg = bank()
                mm(pg, pg[:], sgx, sgx[:], g2b, g2b[:])
                op("act", lambda e: e.activation(out=Gkeep[par][:], in_=pg[:], func=AF.Copy), reads=[pg], writes=[Gkeep[par]])
            op("pool", lambda e: e.tensor_tensor(out=kk[:], in0=Ks[:], in1=KKB[:], op=ALU.mult), reads=[Ks, KKB], writes=[kk])
            op("pool", lambda e: e.tensor_tensor(out=kk2[:], in0=kk[:], in1=kk[:], op=ALU.mult), reads=[kk], writes=[kk2])
            op("dve", lambda e: e.tensor_reduce(out=s8[:], in_=b3(kk2[:]), axis=AX.X, op=ALU.add), reads=[kk2], writes=[s8])
            op("act", lambda e: e.activation(out=s8[:], in_=s8[:], func=AF.Sqrt), reads=[s8], writes=[s8])
            op("dve", lambda e: e.tensor_scalar(out=s8[:], in0=s8[:], scalar1=1e-12, scalar2=None, op0=ALU.max), reads=[s8], writes=[s8])
            op("dve", lambda e: e.reciprocal(out=rn8[:], in_=s8[:]), reads=[s8], writes=[rn8])
            op("dve", lambda e: e.tensor_tensor(out=b3(kkn[:]), in0=b3(kk[:]), in1=bc8(rn8[:]), op=ALU.mult), reads=[kk, rn8], writes=[kkn])
            op("dve", lambda e: e.scalar_tensor_tensor(out=t1[:], in0=av[:], scalar=-1.0, in1=KAB[:], op0=ALU.add, op1=ALU.mult),
               reads=[av, KAB], writes=[t1])
            op("dve", lambda e: e.scalar_tensor_tensor(out=kp[:], in0=t1[:], scalar=1.0, in1=Ks[:], op0=ALU.add, op1=ALU.mult),
               reads=[t1, Ks], writes=[kp])
            op("dve", lambda e: e.tensor_tensor(out=bq[:], in0=kkn[:], in1=av[:], op=ALU.mult), reads=[kkn, av], writes=[bq])
            p1, p2, p3 = bank(), bank(), bank()
            mm(p1, p1[:], UI, UI[:], sg, sg[:])
            mm(p2, p2[:], US, US[:], sg, sg[:])
            mm(p3, p3[:], LS, LS[:], sg, sg[:])
            op("act", lambda e: e.activation(out=eL[:], in_=p1[:], func=AF.Exp, scale=-C0), reads=[p1], writes=[eL])
            op("act", lambda e: e.activation(out=eNL[:], in_=p1[:], func=AF.Exp, scale=C0), reads=[p1], writes=[eNL])
            op("act", lambda e: e.activation(out=eLex[:], in_=p2[:], func=AF.Exp, scale=-C0), reads=[p2], writes=[eLex])
            op("act", lambda e: e.activation(out=eD[:], in_=p3[:], func=AF.Exp, scale=-C0), reads=[p3], writes=[eD])
            pw = bank()
            for h in range(8):
                mm(pw, pw[0:64, h:h + 1], sg, sg[:, h * 64:(h + 1) * 64], ones_c, ones_c[:])
            op("act", lambda e: e.activation(out=WC[par][:], in_=pw[0:64, 0:8], func=AF.Exp, scale=-C0), reads=[pw], writes=[WC[par]])
            op("dve", lambda e: e.tensor_tensor(out=rt[:], in0=Rs[:], in1=eL[:], op=ALU.mult), reads=[Rs, eL], writes=[rt])
            op("dve", lambda e: e.scalar_tensor_tensor(out=at[:], in0=kkn[:], scalar=-1.0, in1=eLex[:], op0=ALU.mult, op1=ALU.mult),
               reads=[kkn, eLex], writes=[at])
            op("dve", lambda e: e.tensor_tensor(out=bt[:], in0=bq[:], in1=eNL[:], op=ALU.mult), reads=[bq, eNL], writes=[bt])
            op("pool", lambda e: e.tensor_tensor(out=kt[:], in0=kp[:], in1=eNL[:], op=ALU.mult), reads=[kp, eNL], writes=[kt])
            op("dve", lambda e: e.tensor_tensor(out=Bh[par][:], in0=bq[:], in1=eD[:], op=ALU.mult), reads=[bq, eD], writes=[Bh[par]])
            op("pool", lambda e: e.tensor_tensor(out=Kh[par][:], in0=kp[:], in1=eD[:], op=ALU.mult), reads=[kp, eD], writes=[Kh[par]])
            if own:
                op("pool", lambda e: e.tensor_tensor(out=rk[:], in0=Rs[:], in1=kp[:], op=ALU.mult), reads=[Rs, kp], writes=[rk])
                op("pool", lambda e: e.tensor_tensor(out=rk[:], in0=rk[:], in1=RKB[:], op=ALU.mult), reads=[rk, RKB], writes=[rk])
                op("dve", lambda e: e.tensor_reduce(out=rs8[par][:], in_=b3(rk[:]), axis=AX.X, op=ALU.add), reads=[rk], writes=[rs8[par]])
                op("pool", lambda e: e.tensor_copy(out=Vkeep[par][:], in_=Vs[:]), reads=[Vs], writes=[Vkeep[par]])
            for ((srcA, dstA), (srcB, dstB)) in (((rt, rtT[par]), (at, atT[par])), ((bt, btT), (kt, ktT))):
                for q_, src in enumerate((srcA, srcB)):
                    for pr in range(4):
                        op("pe", lambda e: e.transpose(out=PT[:, (q_ * 4 + pr) * 128:(q_ * 4 + pr + 1) * 128], in_=src[:, pr * 128:(pr + 1) * 128],
                                                       identity=idb[:]), reads=[src, idb], writes=[PT])
                for q_, dst in enumerate((dstA, dstB)):
                    dv = dst[:].rearrange("p (a b t) -> p a b t", a=4, b=2)
                    op("act", lambda e: e.activation(out=dv[:, :, 0, :], in_=PT[0:64, q_ * 512:(q_ + 1) * 512].rearrange("p (a t) -> p a t", a=4),
                                                     func=AF.Copy), reads=[PT], writes=[dst])
                    op("dve", lambda e: e.tensor_copy(out=dv[:, :, 1, :], in_=PT[64:128, q_ * 512:(q_ + 1) * 512].rearrange("p (a t) -> p a t", a=4)),
                       reads=[PT], writes=[dst])
            def cmat(lT, rT, mask, dsts):
                for grp in range(2):
                    pb2 = bank()
                    for hh in range(4):
                        h = grp * 4 + hh
                        mm(pb2, pb2[:, hh * 128:(hh + 1) * 128], lT, lT[:, h * 128:(h + 1) * 128], rT, rT[:, h * 128:(h + 1) * 128])
                    d = dsts[grp]
                    op("dve", lambda e: e.tensor_tensor(out=d[:].rearrange("p (h t) -> p h t", h=4),
                                                        in0=pb2[:].rearrange("p (h t) -> p h t", h=4),
                                                        in1=mask[:].unsqueeze(1).to_broadcast([128, 4, 128]), op=ALU.mult),
                       reads=[pb2, mask], writes=[d])
            cmat(atT[par], btT, LS, Nm)
            cmat(btT, atT[par], US, Mm)
            cmat(ktT, atT[par], US, AKt[par])
            if own:
                cmat(btT, rtT[par], UI, RBt[par])
                cmat(ktT, rtT[par], UI, RKt[par])
            for grp in range(2):
                P = Pm[par][grp]
                op("dve", lambda e: e.tensor_tensor(out=P[:].rearrange("p (h t) -> p h t", h=4),
                                                    in0=Mm[grp][:].rearrange("p (h t) -> p h t", h=4),
                                                    in1=idf[:].unsqueeze(1).to_broadcast([128, 4, 128]), op=ALU.add),
                   reads=[Mm[grp], idf], writes=[P])
            cn, cm = Nm, Mm
            nn, nm = N2, M2
            for lvl in range(6):
                last = lvl == 5
                for grp in range(2):
                    pn = bank()
                    for hh in range(4):
                        sl = slice(hh * 128, (hh + 1) * 128)
                        mm(pn, pn[:, sl], cm[grp], cm[grp][:, sl], cn[grp], cn[grp][:, sl])
                    op("act", lambda e: e.activation(out=nn[grp][:], in_=pn[:], func=AF.Copy), reads=[pn], writes=[nn[grp]])
                    if not last:
                        pm_ = bank()
                        for hh in range(4):
                            sl = slice(hh * 128, (hh + 1) * 128)
                            mm(pm_, pm_[:, sl], cn[grp], cn[grp][:, sl], cm[grp], cm[grp][:, sl])
                        op("act", lambda e: e.activation(out=nm[grp][:], in_=pm_[:], func=AF.Copy), reads=[pm_], writes=[nm[grp]])
                    P = Pm[par][grp]
                    pp = bank()
                    for hh in range(4):
                        sl = slice(hh * 128, (hh + 1) * 128)
                        mm(pp, pp[:, sl], nn[grp], nn[grp][:, sl], P, P[:, sl])
                    op("dve", lambda e: e.tensor_tensor(out=P[:], in0=pp[:], in1=P[:], op=ALU.add), reads=[pp, P], writes=[P])
                cn, cm, nn, nm = nn, nm, cn, cm
                if sgen is not None:
                    next(sgen, None)

        def seq(tt):
            par = tt % 2
            own = tt >= OWN0 // 128 - 1
            pg = bank()
            for h in range(8):
                grp, hh = h // 4, h % 4
                hs = slice(h * 64, (h + 1) * 64)
                mm(pg, pg[:, hs], atT[par], atT[par][:, h * 128:(h + 1) * 128], STb, STb[:, hs], start=True, stop=False)
                mm(pg, pg[:, hs], AKt[par][grp], AKt[par][grp][:, hh * 128:(hh + 1) * 128], Vb[par], Vb[par][:, hs], start=False, stop=True)
            op("act", lambda e: e.activation(out=Gb[:], in_=pg[:], func=AF.Copy), reads=[pg], writes=[Gb])
            yield
            pu = bank()
            for h in range(8):
                grp, hh = h // 4, h % 4
                hs = slice(h * 64, (h + 1) * 64)
                mm(pu, pu[:, hs], Pm[par][grp], Pm[par][grp][:, hh * 128:(hh + 1) * 128], Gb, Gb[:, hs])
            op("dve", lambda e: e.tensor_copy(out=Ub[:], in_=pu[:]), reads=[pu], writes=[Ub])
            yield
            if own:
                py = bank()
                for h in range(8):
                    grp, hh = h // 4, h % 4
                    hs = slice(h * 64, (h + 1) * 64)
                    mm(py, py[:, hs], rtT[par], rtT[par][:, h * 128:(h + 1) * 128], STb, STb[:, hs], start=True, stop=False)
                    mm(py, py[:, hs], RBt[par][grp], RBt[par][grp][:, hh * 128:(hh + 1) * 128], Ub, Ub[:, hs], start=False, stop=False)
                    mm(py, py[:, hs], RKt[par][grp], RKt[par][grp][:, hh * 128:(hh + 1) * 128], Vb[par], Vb[par][:, hs], start=False, stop=True)
                op("act", lambda e: e.activation(out=ysb[:], in_=py[:], func=AF.Copy), reads=[py], writes=[ysb])
            yield
            pn = bank()
            for h in range(8):
                hs = slice(h * 64, (h + 1) * 64)
                mm(pn, pn[0:64, hs], Bh[par], Bh[par][:, hs], Ub, Ub[:, hs], start=True, stop=False)
                mm(pn, pn[0:64, hs], Kh[par], Kh[par][:, hs], Vb[par], Vb[par][:, hs], start=False, stop=True)
            op("pool", lambda e: e.tensor_tensor(out=b3(stmp[:]), in0=b3(ST[:]), in1=bc8(WC[par][:], 64), op=ALU.mult),
               reads=[ST, WC[par]], writes=[stmp])
            op("dve", lambda e: e.tensor_tensor(out=ST[:], in0=pn[0:64, :], in1=stmp[:], op=ALU.add), reads=[pn, stmp], writes=[ST])
            op("act", lambda e: e.activation(out=STb[:], in_=ST[:], func=AF.Copy), reads=[ST], writes=[STb])
            yield
            if own:
                op("dve", lambda e: e.tensor_reduce(out=m8[:], in_=b3(ysb[:]), axis=AX.X, op=ALU.add), reads=[ysb], writes=[m8])
                op("pool", lambda e: e.tensor_tensor(out=ysq[:], in0=ysb[:], in1=ysb[:], op=ALU.mult), reads=[ysb], writes=[ysq])
                op("dve", lambda e: e.tensor_reduce(out=v8[:], in_=b3(ysq[:]), axis=AX.X, op=ALU.add), reads=[ysq], writes=[v8])
                op("dve", lambda e: e.tensor_scalar(out=m8[:], in0=m8[:], scalar1=1.0 / 64, scalar2=None, op0=ALU.mult), reads=[m8], writes=[m8])
                op("dve", lambda e: e.tensor_tensor(out=r8[:], in0=m8[:], in1=m8[:], op=ALU.mult), reads=[m8], writes=[r8])
                op("dve", lambda e: e.scalar_tensor_tensor(out=v8[:], in0=v8[:], scalar=1.0 / 64, in1=r8[:], op0=ALU.mult, op1=ALU.subtract),
                   reads=[v8, r8], writes=[v8])
                op("dve", lambda e: e.tensor_scalar(out=v8[:], in0=v8[:], scalar1=64e-5, scalar2=None, op0=ALU.add), reads=[v8], writes=[v8])
                op("act", lambda e: e.activation(out=v8[:], in_=v8[:], func=AF.Sqrt), reads=[v8], writes=[v8])
                op("dve", lambda e: e.reciprocal(out=r8[:], in_=v8[:]), reads=[v8], writes=[r8])
                op("dve", lambda e: e.tensor_tensor(out=b3(yn[:]), in0=b3(ysb[:]), in1=bc8(m8[:]), op=ALU.subtract), reads=[ysb, m8], writes=[yn])
                op("dve", lambda e: e.tensor_tensor(out=b3(yn[:]), in0=b3(yn[:]), in1=bc8(r8[:]), op=ALU.mult), reads=[yn, r8], writes=[yn])
                op("pool", lambda e: e.tensor_tensor(out=yn[:], in0=yn[:], in1=LNW[:], op=ALU.mult), reads=[yn, LNW], writes=[yn])
                op("pool", lambda e: e.tensor_tensor(out=yn[:], in0=yn[:], in1=LNB[:], op=ALU.add), reads=[yn, LNB], writes=[yn])
                op("dve", lambda e: e.tensor_tensor(out=b3(ysq[:]), in0=b3(Vkeep[par][:]), in1=bc8(rs8[par][:]), op=ALU.mult),
                   reads=[Vkeep[par], rs8[par]], writes=[ysq])
                op("pool", lambda e: e.tensor_tensor(out=yn[:], in0=yn[:], in1=ysq[:], op=ALU.add), reads=[yn, ysq], writes=[yn])
                if dbg_o:
                    op("pool", lambda e: e.tensor_tensor(out=ysq[:], in0=yn[:], in1=Gkeep[par][:], op=ALU.mult), reads=[yn, Gkeep[par]], writes=[ysq])
                    kb.dma(dbg_o["orw"][(tt - 47) * 128:(tt - 46) * 128, :], ysq[:], reads=[ysq])
                op("pool", lambda e: e.tensor_tensor(out=ob16[:], in0=yn[:], in1=Gkeep[par][:], op=ALU.mult), reads=[yn, Gkeep[par]], writes=[ob16])
                for c in range(4):
                    op("pe", lambda e: e.transpose(out=PT[:, c * 128:(c + 1) * 128], in_=ob16[:, c * 128:(c + 1) * 128], identity=idb[:]),
                       reads=[ob16, idb], writes=[PT])
                op("act", lambda e: e.activation(out=oT[:], in_=PT[:, 0:512], func=AF.Copy), reads=[PT], writes=[oT])
                for c in range(4):
                    kb.dma(ORT[c * 128:(c + 1) * 128, (tt - 47) * 128:(tt - 46) * 128], oT[:, c * 128:(c + 1) * 128], reads=[oT])

        t0 = 64 - ntiles
        for tt in range(t0, 64):
            sgen = seq(tt - 1) if tt > t0 else None
            prep(tt, sgen)
            if sgen is not None:
                for _ in sgen:
                    pass
        for _ in seq(63):
            pass
        if dbg_o:
            kb.dma(dbg_o["ks2"], KS2[0], eng="sp")
            kb.dma(dbg_o["vsw"], VSW, eng="sp")


def phase3(nc, kb, st0, G):
    op = kb.op
    mm = G["mm"]
    PT = G["PT"]
    banks = G["banks"]
    idf, idb, UI, US, LS, BO = G["idf"], G["idb"], G["UI"], G["US"], G["LS"], G["BO"]
    dbg_o = G["dbg_o"]
    KS2, KW2, KCT, VCT, VSW, HTO, ONT = G["KS2"], G["KW2"], G["KCT"], G["VCT"], G["VSW"], G["HTO"], G["ONT"]
    g = lambda k: G[k]
    rot = banks[0:3]
    ACC0, ACC1 = banks[5], banks[6]
    MES = [banks[3], banks[4], banks[6]]
    ri = [0]
    mi = [0]

    def bank():
        b = rot[ri[0] % 3]
        ri[0] += 1
        return b

    with ExitStack() as st:
        def sb(shape, dt):
            return kb.sb(shape, dt, st)
        KSs = [sb([128, NT], BF16) for _ in range(2)]
        VS1 = [sb([128, 64, 128], BF16) for _ in range(2)]
        KWs = [sb([128, 22 * 128], BF16) for _ in range(2)]
        VW1 = [sb([128, 22, 128], BF16) for _ in range(2)]
        BIGE = sb([128, NT], BF16)
        for gi in range(2):
            kb.dma(KSs[gi][0:64, :], KS2[gi][0:64, :], writes=[KSs[gi]])
            kb.dma(KWs[gi][0:64, :], KW2[gi][0:64, 42 * 128:64 * 128], writes=[KWs[gi]])
        for gi in range(2):
            for j0 in range(0, 64, 8):
                kb.dma(VS1[gi][:, j0:j0 + 8, 0:64], VSW[j0 * 128:(j0 + 8) * 128, gi * 64:(gi + 1) * 64].rearrange("(j p) d -> p j d", p=128), writes=[VS1[gi]])
            for j0 in range(0, 64, 16):
                kb.dma(VS1[gi][:, j0:j0 + 16, 64:128], g("onesv")[:, j0:j0 + 16, :], writes=[VS1[gi]])
            for j0 in range(0, 22, 11):
                kb.dma(VW1[gi][:, j0:j0 + 11, 0:64], VSW[(42 + j0) * 128:(53 + j0) * 128, 128 + gi * 64:128 + (gi + 1) * 64].rearrange("(j p) d -> p j d", p=128), writes=[VW1[gi]])
            kb.dma(VW1[gi][:, :, 64:128], g("onesv")[:, 42:64, :], writes=[VW1[gi]])
        kb.dma(BIGE[:], g("bige"), writes=[BIGE])
        KCf = [sb([128, 512], F32) for _ in range(2)]
        VCf = [sb([128, 4, 128], F32) for _ in range(2)]
        for gi in range(2):
            op("pool", lambda e: e.memset(VCf[gi][:, :, 64:128], 1.0), writes=[VCf[gi]])
        OVL = sb([128, 4, 132], F32)
        kb.dma(OVL[:], g("ovl"), writes=[OVL])
        CVO = sb([128, 4], F32)
        kb.dma(CVO[:], g("cvoid"), writes=[CVO])
        B31 = [sb([128, 512], F32) for _ in range(2)]
        EMD = [sb([128, 512], BF16) for _ in range(2)]
        EMS = [sb([128, 512], BF16) for _ in range(2)]
        ones_f = sb([128, 128], F32)
        op("pool", lambda e: e.memset(ones_f[:], 1.0), writes=[ones_f])
        UIb = sb([128, 128], BF16)
        LSb = sb([128, 128], BF16)
        op("dve", lambda e: e.tensor_copy(out=UIb[:], in_=UI[:]), reads=[UI], writes=[UIb])
        op("dve", lambda e: e.tensor_copy(out=LSb[:], in_=LS[:]), reads=[LS], writes=[LSb])
        tmpf = sb([128, 512], F32)
        for gi in range(2):
            kb.dma(B31[gi][:], g("b31t")[gi], writes=[B31[gi]])
            for src, dst, msk in ((g("bt_diag"), EMD, UI), (g("bt_sub"), EMS, None)):
                kb.dma(tmpf[:], src[gi], writes=[tmpf])
                op("dve", lambda e: e.tensor_tensor(out=tmpf[:], in0=tmpf[:], in1=B31[gi][:], op=ALU.subtract), reads=[tmpf, B31[gi]], writes=[tmpf])
                op("act", lambda e: e.activation(out=tmpf[:], in_=tmpf[:], func=AF.Exp), reads=[tmpf], writes=[tmpf])
                if msk is not None:
                    op("dve", lambda e: e.tensor_tensor(out=dst[gi][:].rearrange("p (h q) -> p h q", h=4), in0=tmpf[:].rearrange("p (h q) -> p h q", h=4),
                                                        in1=msk[:].unsqueeze(1).to_broadcast([128, 4, 128]), op=ALU.mult), reads=[tmpf, msk], writes=[dst[gi]])
                else:
                    op("dve", lambda e: e.tensor_copy(out=dst[gi][:], in_=tmpf[:]), reads=[tmpf], writes=[dst[gi]])
        WQ = sb([128, 8, 512], BF16)
        WG = sb([128, 8, 24], BF16)
        GSEL = sb([24, 24 * 64], BF16)
        kb.dma(GSEL[:], g("gsel"), writes=[GSEL])
        QNG = sb([128, 1], F32)
        kb.dma(QNG[:], g("qng"), writes=[QNG])
        op("dve", lambda e: e.tensor_scalar(out=QNG[:], in0=QNG[:], scalar1=0.125, scalar2=None, op0=ALU.mult), reads=[QNG], writes=[QNG])
        for kc in range(8):
            kb.dma(tmpf[:], g("w_q")[kc * 128:(kc + 1) * 128, :], writes=[tmpf])
            op("dve", lambda e: e.tensor_copy(out=WQ[:, kc, :], in_=tmpf[:]), reads=[tmpf], writes=[WQ])
            kb.dma(tmpf[:, 0:24], g("w_g")[kc * 128:(kc + 1) * 128, :], writes=[tmpf])
            op("dve", lambda e: e.tensor_copy(out=WG[:, kc, :], in_=tmpf[:, 0:24]), reads=[tmpf], writes=[WG])

        with ExitStack() as st2:
            def sb2(shape, dt):
                return kb.sb(shape, dt, st2)
            kvb = [sb2([64, NT], BF16) for _ in range(2)]
            kvT = [kvb, kvb]
            W1 = sb2([128, 32, 128], BF16)
            W1f = sb2([128, 32, 128], F32)
            POS = sb2([128, 32], F32)
            W2f = sb2([128, 64], F32)
            W2d = sb2([128, 128], F32)
            CB1 = sb2([128, 2], F32)
            kb.dma(CB1[:], g("cb1"), writes=[CB1])
            CB2K = sb2([128, 1], F32)
            kb.dma(CB2K[:], g("cb2k"), writes=[CB2K])
            CB2V = sb2([128, 64], F32)
            kb.dma(CB2V[:], g("cb2v").partition_broadcast(128), writes=[CB2V])
            KCG = sb2([128, 1], F32)
            kb.dma(KCG[:], g("kcg"), writes=[KCG])
            cst = sb2([128, 1], F32)
            xh = sb2([128, 512], F32)
            x2 = sb2([128, 512], F32)
            ge = sb2([128, 512], F32)
            for kv in range(2):
                for gi in range(2):
                    kb.dma(kvb[gi][:], (KCT if kv == 0 else VCT)[gi * 64:(gi + 1) * 64, :], writes=[kvb[gi]])
                kb.dma(W1f[:], g("cw1")[kv], writes=[W1f])
                op("dve", lambda e: e.tensor_copy(out=W1[:], in_=W1f[:]), reads=[W1f], writes=[W1])
                kb.dma(POS[:], g("cpos")[kv], writes=[POS])
                kb.dma(W2f[:], g("cw2")[kv], writes=[W2f])
                op("dve", lambda e: e.tensor_copy(out=W2d[:, 0:64], in_=W2f[:]), reads=[W2f], writes=[W2d])
                op("dve", lambda e: e.tensor_copy(out=W2d[:, 64:128], in_=W2f[:]), reads=[W2f], writes=[W2d])
                pc = bank()
                for l in range(32):
                    mm(pc, pc[:, 0:1], W1f, W1f[0:64, l, :], POS, POS[0:64, l:l + 1], start=(l == 0), stop=(l == 31))
                op("dve", lambda e: e.tensor_tensor(out=cst[:], in0=pc[:, 0:1], in1=CB1[:, kv:kv + 1], op=ALU.add), reads=[pc, CB1], writes=[cst])
                for gi in range(2):
                    ph = bank()
                    for l in range(32):
                        mm(ph, ph[:, 0:511], W1, W1[0:64, l, :], kvT[kv][gi], kvT[kv][gi][:, l:l + 16 * 510 + 1:16],
                           start=(l == 0), stop=(l == 31))
                    op("pool", lambda e: e.memset(xh[:, 511:512], 0.0), writes=[xh])
                    op("act", lambda e: e.activation(out=xh[:, 0:511], in_=ph[:, 0:511], func=AF.Identity, bias=cst[:, 0:1]), reads=[ph, cst], writes=[xh])
                    op("pool", lambda e: e.tensor_tensor(out=x2[:], in0=xh[:], in1=xh[:], op=ALU.mult), reads=[xh], writes=[x2])
                    op("dve", lambda e: e.tensor_scalar(out=x2[:], in0=x2[:], scalar1=0.044715, scalar2=1.0, op0=ALU.mult, op1=ALU.add), reads=[x2], writes=[x2])
                    op("pool", lambda e: e.tensor_tensor(out=x2[:], in0=x2[:], in1=xh[:], op=ALU.mult), reads=[x2, xh], writes=[x2])
                    op("act", lambda e: e.activation(out=x2[:], in_=x2[:], func=AF.Tanh, scale=0.7978845608028654), reads=[x2], writes=[x2])
                    op("dve", lambda e: e.scalar_tensor_tensor(out=ge[:], in0=x2[:], scalar=1.0, in1=xh[:], op0=ALU.add, op1=ALU.mult), reads=[x2, xh], writes=[ge])
                    op("dve", lambda e: e.tensor_scalar(out=ge[:], in0=ge[:], scalar1=0.5, scalar2=None, op0=ALU.mult), reads=[ge], writes=[ge])
                    if kv == 0:
                        po = bank()
                        mm(po, po[:], W2d, W2d[:], ge, ge[:])
                        op("act", lambda e: e.activation(out=xh[:], in_=po[:], func=AF.Identity, bias=CB2K[:, 0:1]), reads=[po, CB2K], writes=[xh])
                        op("pool", lambda e: e.tensor_tensor(out=x2[:], in0=xh[:], in1=xh[:], op=ALU.mult), reads=[xh], writes=[x2])
                        pq = bank()
                        mm(pq, pq[:], BO, BO[:], x2, x2[:])
                        op("dve", lambda e: e.tensor_scalar(out=x2[:], in0=pq[:], scalar1=1.0 / 64, scalar2=1e-6, op0=ALU.mult, op1=ALU.add), reads=[pq], writes=[x2])
                        op("act", lambda e: e.activation(out=x2[:], in_=x2[:], func=AF.Sqrt), reads=[x2], writes=[x2])
                        op("dve", lambda e: e.reciprocal(out=x2[:], in_=x2[:]), reads=[x2], writes=[x2])
                        op("dve", lambda e: e.scalar_tensor_tensor(out=KCf[gi][:], in0=xh[:], scalar=KCG[:, 0:1], in1=x2[:], op0=ALU.mult, op1=ALU.mult),
                           reads=[xh, KCG, x2], writes=[KCf[gi]])
                        if dbg_o:
                            kb.dma(dbg_o["kc"][gi], KCf[gi][:], reads=[KCf[gi]])
                    else:
                        for ct in range(4):
                            po = bank()
                            mm(po, po[:, 0:64], ge, ge[:, ct * 128:(ct + 1) * 128], W2f, W2f[:])
                            op("dve", lambda e: e.tensor_tensor(out=VCf[gi][:, ct, 0:64], in0=po[:, 0:64], in1=CB2V[:], op=ALU.add), reads=[po, CB2V], writes=[VCf[gi]])
                        if dbg_o:
                            kb.dma(dbg_o["vc"][gi], VCf[gi][:, :, 0:64], reads=[VCf[gi]])
            kb.barrier()

        hto = sb([128, 8, 128], BF16)
        qsq = sb([128, 512], F32)
        qrs = sb([128, 512], F32)
        QNf = sb([128, 512], F32)
        QNb = sb([128, 512], BF16)
        gT = sb([24, 128], BF16)
        GBs = [sb([64, 3, 512], F32) for _ in range(2)]
        gbi = [0]
        PC = [sb([128, 512], F32) for _ in range(4)]
        bmt = [sb([128, 512], F32) for _ in range(2)]
        rden = sb([128, 512], F32)
        sk = sb([128, 128], F32)
        sa = sb([128, 128], F32)
        sv = sb([128, 128], F32)
        sc = sb([128, 128], F32)
        sc2 = sb([128, 128], F32)
        m8a = sb([128, 8], F32)
        m8b = sb([128, 8], F32)
        smb = sb([128, 128], BF16)
        mT = sb([128, 128], BF16)
        Eb = [sb([128, 512], BF16) for _ in range(4)]
        Pb = [sb([128, 512], BF16) for _ in range(4)]
        oacc = sb([64, 512], F32)
        otmp = sb([64, 512], F32)
        rd64 = sb([64, 512], F32)
        accs = [[sb([128, 512], F32) for _ in range(3)] for _ in range(2)]
        acs = [0]
        epg = [None]
        rdq = sb([128, 4], F32)
        o16 = sb([64, 512], BF16)
        ei = [0]

        def qhalf(r):
            return slice((r % 2) * 64, (r % 2) * 64 + 64), slice((r // 2) * 128, (r // 2) * 128 + 128)

        def finish_branch(br, first):
            op("act", lambda e: e.activation(out=accs[acs[0] % 2][br][:], in_=ACC0[:], func=AF.Copy), reads=[ACC0], writes=[accs[acs[0] % 2][br]])

        def epilogue(GBc, acc3, qi_, gi_):
            for br in range(3):
                A = acc3[br]
                op("dve", lambda e: e.tensor_scalar(out=rd64[:], in0=A[64:128, :], scalar1=1e-30, scalar2=None, op0=ALU.max), reads=[A], writes=[rd64])
                yield
                op("dve", lambda e: e.reciprocal(out=rd64[:], in_=rd64[:]), reads=[rd64], writes=[rd64])
                yield
                op("pool", lambda e: e.tensor_tensor(out=rd64[:], in0=rd64[:], in1=GBc[:, br, :], op=ALU.mult), reads=[rd64, GBc], writes=[rd64])
                yield
                if br == 0:
                    op("pool", lambda e: e.tensor_tensor(out=oacc[:], in0=A[0:64, :], in1=rd64[:], op=ALU.mult), reads=[A, rd64], writes=[oacc])
                    yield
                else:
                    op("pool", lambda e: e.tensor_tensor(out=otmp[:], in0=A[0:64, :], in1=rd64[:], op=ALU.mult), reads=[A, rd64], writes=[otmp])
                    yield
                    dst_ = o16 if br == 2 else oacc
                    op("pool", lambda e: e.tensor_tensor(out=dst_[:], in0=oacc[:], in1=otmp[:], op=ALU.add), reads=[oacc, otmp], writes=[dst_])
                    yield
            kb.dma(ONT[gi_ * 256:(gi_ + 1) * 256, qi_ * 128:(qi_ + 1) * 128].rearrange("(r d) q -> d r q", d=64),
                   o16[:].rearrange("d (r q) -> d r q", r=4), reads=[o16], eng="pool")
            yield

        for qi in range(G["p3tiles"]):
            tq = 47 + qi
            kb.dma(hto[:], HTO[:, :, qi * 128:(qi + 1) * 128], writes=[hto])
            kb.dma(sk[:], g("selkeep")[qi], writes=[sk])
            kb.dma(sa[:], g("seladd")[qi], writes=[sa])
            kb.dma(sv[:], g("selv")[qi], writes=[sv])
            pg = bank()
            for kc in range(8):
                mm(pg, pg[0:24, 0:128], WG, WG[:, kc, :], hto, hto[:, kc, :], start=(kc == 0), stop=(kc == 7))
            op("act", lambda e: e.activation(out=qsq[0:24, 0:128], in_=pg[0:24, 0:128], func=AF.Tanh, scale=0.5), reads=[pg], writes=[qsq])
            op("dve", lambda e: e.tensor_scalar(out=gT[:], in0=qsq[0:24, 0:128], scalar1=0.5, scalar2=0.5, op0=ALU.mult, op1=ALU.add), reads=[qsq], writes=[gT])
            for gi in range(2):
                GB = GBs[gbi[0] % 2]
                gbi[0] += 1
                for br in range(3):
                    pgb = bank()
                    for r in range(4):
                        j = (gi * 4 + r) * 3 + br
                        mm(pgb, pgb[0:64, r * 128:(r + 1) * 128], GSEL, GSEL[:, j * 64:(j + 1) * 64], gT, gT[:])
                    op("act", lambda e: e.activation(out=GB[:, br, :], in_=pgb[0:64, :], func=AF.Copy), reads=[pgb], writes=[GB])

                pq = bank()
                for r in range(4):
                    c0 = (gi * 4 + r) * 64
                    for kc in range(8):
                        mm(pq, pq[0:64, r * 128:(r + 1) * 128], WQ, WQ[:, kc, c0:c0 + 64], hto, hto[:, kc, :], start=(kc == 0), stop=(kc == 7))
                op("act", lambda e: e.activation(out=qsq[0:64, :], in_=pq[0:64, :], func=AF.Square), reads=[pq], writes=[qsq])
                pq2 = bank()
                mm(pq2, pq2[0:64, :], ones_f, ones_f[0:64, 0:64], qsq, qsq[0:64, :])
                op("dve", lambda e: e.tensor_scalar(out=qrs[0:64, :], in0=pq2[0:64, :], scalar1=1.0 / 64, scalar2=1e-6, op0=ALU.mult, op1=ALU.add), reads=[pq2], writes=[qrs])
                op("act", lambda e: e.activation(out=qrs[0:64, :], in_=qrs[0:64, :], func=AF.Ln), reads=[qrs], writes=[qrs])
                op("act", lambda e: e.activation(out=qrs[0:64, :], in_=qrs[0:64, :], func=AF.Exp, scale=-0.5), reads=[qrs], writes=[qrs])
                op("dve", lambda e: e.scalar_tensor_tensor(out=QNf[0:64, :], in0=pq[0:64, :], scalar=QNG[0:64, 0:1], in1=qrs[0:64, :], op0=ALU.mult, op1=ALU.mult),
                   reads=[pq, QNG, qrs], writes=[QNf])
                op("pool", lambda e: e.tensor_copy(out=QNb[0:64, :], in_=QNf[0:64, :]), reads=[QNf], writes=[QNb])
                if P3LVL < 2:
                    continue
                cts = [ct for ct in range(4) if tq - 16 * ct >= 0]
                for ii, ct in enumerate(cts):
                    m = tq - 16 * ct
                    ps_ = bank()
                    mm(ps_, ps_[:], KCf[gi], KCf[gi][0:64, ct * 128:(ct + 1) * 128], QNf, QNf[0:64, :])
                    if m <= 17:
                        bt_ = bmt[ii % 2]
                        kb.dma(bt_[:], g("bmc")[m, gi], writes=[bt_])
                    else:
                        bt_ = B31[gi]
                    op("dve", lambda e: e.tensor_tensor(out=PC[ct][:], in0=ps_[:], in1=bt_[:], op=ALU.add), reads=[ps_, bt_], writes=[PC[ct]])
                    op("act", lambda e: e.activation(out=PC[ct][:], in_=PC[ct][:], func=AF.Exp, bias=CVO[:, ct:ct + 1]), reads=[PC[ct], CVO], writes=[PC[ct]])
                for ii, ct in enumerate(cts):
                    mm(ACC0, ACC0[:], VCf[gi], VCf[gi][:, ct, :], PC[ct], PC[ct][:], start=(ii == 0), stop=(ii == len(cts) - 1))
                if P3SUB < 3:
                    continue
                finish_branch(0, True)
                if P3LVL < 3:
                    continue
                pis = [bank(), bank()]
                for r in range(4):
                    pb_ = pis[r // 2]
                    c0_ = (r % 2) * 132
                    for ii, ct in enumerate(cts):
                        mm(pb_, pb_[:, c0_:c0_ + 129], PC[ct], PC[ct][:, r * 128:(r + 1) * 128], OVL, OVL[:, ct, 0:129],
                           start=(ii == 0), stop=(ii == len(cts) - 1))
                for r in range(4):
                    pb_ = pis[r // 2]
                    c0_ = (r % 2) * 132
                    op("dve", lambda e: e.tensor_scalar(out=rdq[:, r:r + 1], in0=pb_[:, c0_ + 128:c0_ + 129], scalar1=1e-30, scalar2=None, op0=ALU.max),
                       reads=[pb_], writes=[rdq])
                op("dve", lambda e: e.reciprocal(out=rdq[:], in_=rdq[:]), reads=[rdq], writes=[rdq])
                op("dve", lambda e: e.tensor_scalar(out=sc2[:], in0=pis[0][:, 0:128], scalar1=rdq[:, 0:1], scalar2=None, op0=ALU.mult), reads=[pis[0], rdq], writes=[sc2])
                for r in range(1, 4):
                    pb_ = pis[r // 2]
                    c0_ = (r % 2) * 132
                    op("dve", lambda e: e.scalar_tensor_tensor(out=sc2[:], in0=pb_[:, c0_:c0_ + 128], scalar=rdq[:, r:r + 1], in1=sc2[:],
                                                               op0=ALU.mult, op1=ALU.add), reads=[pb_, rdq, sc2], writes=[sc2])
                op("dve", lambda e: e.tensor_tensor(out=sc[:], in0=sc2[:], in1=sk[:], op=ALU.mult), reads=[sc2, sk], writes=[sc])
                if dbg_o:
                    kb.dma(dbg_o["imp"][qi, gi], sc[:], reads=[sc])
                op("dve", lambda e: e.tensor_tensor(out=sc[:], in0=sc[:], in1=sa[:], op=ALU.add), reads=[sc, sa], writes=[sc])
                op("dve", lambda e: e.max(out=m8a[:], in_=sc[:]), reads=[sc], writes=[m8a])
                op("dve", lambda e: e.match_replace(out=sc2[:], in_to_replace=m8a[:], in_values=sc[:], imm_value=-3.0e38), reads=[sc, m8a], writes=[sc2])
                op("dve", lambda e: e.max(out=m8b[:], in_=sc2[:]), reads=[sc2], writes=[m8b])
                op("dve", lambda e: e.tensor_scalar(out=sc2[:], in0=sc[:], scalar1=m8b[:, 7:8], scalar2=None, op0=ALU.is_ge), reads=[sc, m8b], writes=[sc2])
                op("dve", lambda e: e.tensor_tensor(out=smb[:], in0=sc2[:], in1=sv[:], op=ALU.mult), reads=[sc2, sv], writes=[smb])

                if P3LVL < 4:
                    continue
                def run_steps(steps):
                    pend = []

                    def stage_c(st_, P):
                        mm(ACC0, ACC0[:], st_["vb"], st_["vap"], P, P[:], start=st_["first"], stop=st_["last"])
                    for st_ in steps:
                        ps_ = bank()
                        mm(ps_, ps_[:], st_["kb"], st_["kap"], QNb, QNb[0:64, :])
                        me = None
                        if st_["me"]:
                            me = MES[mi[0] % 3]
                            mi[0] += 1
                            mm(me, me[:, 0:128], BIGE, BIGE[:, st_["j"] * 128:(st_["j"] + 1) * 128], mT, mT[:])
                        E = Eb[ei[0] % 4]
                        P = Pb[ei[0] % 4]
                        ei[0] += 1
                        mk = st_["mask"]
                        if me is None and mk is None:
                            op("act", lambda e: e.activation(out=P[:], in_=ps_[:], func=AF.Exp), reads=[ps_], writes=[P])
                        else:
                            op("act", lambda e: e.activation(out=E[:], in_=ps_[:], func=AF.Exp), reads=[ps_], writes=[E])
                            if me is not None:
                                op("dve", lambda e: e.tensor_tensor(out=P[:].rearrange("p (h q) -> p h q", h=4), in0=E[:].rearrange("p (h q) -> p h q", h=4),
                                                                    in1=me[:, 0:128].unsqueeze(1).to_broadcast([128, 4, 128]), op=ALU.mult), reads=[E, me], writes=[P])
                                if mk is not None:
                                    op("pool", lambda e: e.tensor_tensor(out=P[:], in0=P[:], in1=mk[:], op=ALU.mult), reads=[P, mk], writes=[P])
                            elif mk is LSb:
                                op("pool", lambda e: e.tensor_tensor(out=P[:].rearrange("p (h q) -> p h q", h=4), in0=E[:].rearrange("p (h q) -> p h q", h=4),
                                                                     in1=LSb[:].unsqueeze(1).to_broadcast([128, 4, 128]), op=ALU.mult), reads=[E, LSb], writes=[P])
                            else:
                                op("pool", lambda e: e.tensor_tensor(out=P[:], in0=E[:], in1=mk[:], op=ALU.mult), reads=[E, mk], writes=[P])
                        pend.append((st_, P))
                        if len(pend) > 2:
                            stage_c(*pend.pop(0))
                        if st_["me"] and epg[0] is not None and st_["j"] % 2 == 1:
                            next(epg[0], None)
                    while pend:
                        stage_c(*pend.pop(0))

                steps = []
                for o in range(4, -1, -1):
                    j = tq - o
                    jw = j - 42
                    steps.append(dict(j=j, kb=KWs[gi], kap=KWs[gi][0:64, jw * 128:(jw + 1) * 128], me=False,
                                      mask=(LSb if o == 4 else EMD[gi] if o == 0 else EMS[gi] if o == 1 else None),
                                      vb=VW1[gi], vap=VW1[gi][:, jw, :], first=(o == 4), last=(o == 0)))
                run_steps(steps)
                finish_branch(2, False)
                op("pe", lambda e: e.transpose(out=PT[:, 0:128], in_=smb[:], identity=idb[:]), reads=[smb, idb], writes=[PT])
                op("act", lambda e: e.activation(out=mT[:], in_=PT[:, 0:128], func=AF.Copy), reads=[PT], writes=[mT])
                steps = []
                for j in range(tq + 1):
                    steps.append(dict(j=j, kb=KSs[gi], kap=KSs[gi][0:64, j * 128:(j + 1) * 128], me=True,
                                      mask=(EMD[gi] if j == tq else EMS[gi] if j == tq - 1 else None),
                                      vb=VS1[gi], vap=VS1[gi][:, j, :], first=(j == 0), last=(j == tq)))
                run_steps(steps)
                finish_branch(1, False)
                if epg[0] is not None:
                    for _ in epg[0]:
                        pass
                epg[0] = epilogue(GB, accs[acs[0] % 2], qi, gi)
                acs[0] += 1
        if epg[0] is not None:
            for _ in epg[0]:
                pass
        if dbg_o:
            kb.barrier()
            kb.dma(dbg_o["ont"], ONT)


def phase4(nc, kb, st0, G):
    op = kb.op
    mm = G["mm"]
    bank = G["bank"]
    PT = G["PT"]
    idb = G["idb"]
    xp, w_out, g2n, ffn_up, ffn_down, cw, cb, hmask = (G[k] for k in ("xp", "w_out", "g2n", "ffn_up", "ffn_down", "cw", "cb", "hmask"))
    ORT, ONT, y = G["ORT"], G["ONT"], G["y"]
    X1S, H2S = G["X1S"], G["H2S"]
    ntl = G["p4tiles"]
    with ExitStack() as st:
        def sb(shape, dt):
            return kb.sb(shape, dt, st)
        FU = sb([128, 8, 5632], BF16)
        FD = sb([128, 22, 1024], BF16)
        HIST = sb([128, 44, 2], F32)
        CW = sb([128, 44, 3], F32)
        CB = sb([128, 44], F32)
        HM = sb([128, 1], F32)
        kb.dma(CW[:], cw, writes=[CW])
        kb.dma(CB[:], cb, writes=[CB])
        kb.dma(HM[:], hmask, writes=[HM])
        op("pool", lambda e: e.memset(HIST[:], 0.0), writes=[HIST])
        with ExitStack() as sta:
            def sba(shape, dt):
                return kb.sb(shape, dt, sta)
            WO = sba([128, 8, 1024], BF16)
            G2B = sba([128, 1024], F32)
            kb.dma(G2B[:], g2n.partition_broadcast(128), writes=[G2B])
            stg = [sba([128, 1408], F32) for _ in range(3)]
            n = 0
            engs = ("dve", "pool", "act")

            def cast(dst_b, dst_ap, src_ap, w):
                nonlocal n
                s = stg[n % 3]
                eng = engs[n % 3]
                n += 1
                kb.dma(s[:, 0:w], src_ap, writes=[s])
                if eng == "act":
                    op("act", lambda e: e.activation(out=dst_ap, in_=s[:, 0:w], func=AF.Copy), reads=[s], writes=[dst_b])
                else:
                    op(eng, lambda e: e.tensor_copy(out=dst_ap, in_=s[:, 0:w]), reads=[s], writes=[dst_b])
            for kc in range(8):
                cast(WO, WO[:, kc, :], w_out[kc * 128:(kc + 1) * 128, :], 1024)
            xt = sba([128, 1024], F32)
            x1 = sba([128, 1024], F32)
            xn2 = sba([128, 1024], BF16)
            ss = sba([128, 1], F32)
            h2T = sba([128, 8, 128], BF16)
            oT = sba([128, 8, 128], BF16)
            fu_jobs = [(kc, q) for kc in range(8) for q in range(4)]
            fd_jobs = list(range(22))
            for ti in range(ntl):
                tt = 47 + ti
                kb.dma(xt[:], xp[tt * 128:(tt + 1) * 128, :], writes=[xt])
                kb.dma(oT[:, 0:4, :], ONT[:, ti * 128:(ti + 1) * 128].rearrange("(c p) t -> p c t", p=128), writes=[oT])
                kb.dma(oT[:, 4:8, :], ORT[:, ti * 128:(ti + 1) * 128].rearrange("(c p) t -> p c t", p=128), writes=[oT])
                for hf in range(2):
                    pb = bank()
                    for kc in range(8):
                        mm(pb, pb[:], oT, oT[:, kc, :], WO, WO[:, kc, hf * 512:(hf + 1) * 512], start=(kc == 0), stop=(kc == 7))
                    op("dve", lambda e: e.tensor_tensor(out=x1[:, hf * 512:(hf + 1) * 512], in0=pb[:], in1=xt[:, hf * 512:(hf + 1) * 512], op=ALU.add),
                       reads=[pb, xt], writes=[x1])
                kb.dma(X1S[ti * 128:(ti + 1) * 128, :], x1[:], reads=[x1], eng="pool")
                op("act", lambda e: e.activation(out=xn2[:], in_=x1[:], func=AF.Square, accum_out=ss[:]), reads=[x1], writes=[xn2, ss])
                op("dve", lambda e: e.tensor_scalar(out=ss[:], in0=ss[:], scalar1=1.0 / 1024, scalar2=1e-6, op0=ALU.mult, op1=ALU.add), reads=[ss], writes=[ss])
                op("act", lambda e: e.activation(out=ss[:], in_=ss[:], func=AF.Sqrt), reads=[ss], writes=[ss])
                op("dve", lambda e: e.reciprocal(out=ss[:], in_=ss[:]), reads=[ss], writes=[ss])
                op("dve", lambda e: e.scalar_tensor_tensor(out=xn2[:], in0=x1[:], scalar=ss[:, 0:1], in1=G2B[:], op0=ALU.mult, op1=ALU.mult),
                   reads=[x1, ss, G2B], writes=[xn2])
                for kc in range(8):
                    op("pe", lambda e: e.transpose(out=PT[:, kc * 128:(kc + 1) * 128], in_=xn2[:, kc * 128:(kc + 1) * 128], identity=idb[:]),
                       reads=[xn2, idb], writes=[PT])
                op("act", lambda e: e.activation(out=h2T[:], in_=PT[:].rearrange("p (k t) -> p k t", k=8), func=AF.Copy), reads=[PT], writes=[h2T])
                kb.dma(H2S[:, :, ti * 128:(ti + 1) * 128], h2T[:], reads=[h2T], eng="pool")
                for _ in range(4):
                    if fu_jobs:
                        kc, q = fu_jobs.pop(0)
                        cast(FU, FU[:, kc, q * 1408:(q + 1) * 1408], ffn_up[kc * 128:(kc + 1) * 128, q * 1408:(q + 1) * 1408], 1408)
                for _ in range(2):
                    if fd_jobs:
                        j = fd_jobs.pop(0)
                        cast(FD, FD[:, j, :], ffn_down[j * 128:(j + 1) * 128, :], 1024)
            while fu_jobs:
                kc, q = fu_jobs.pop(0)
                cast(FU, FU[:, kc, q * 1408:(q + 1) * 1408], ffn_up[kc * 128:(kc + 1) * 128, q * 1408:(q + 1) * 1408], 1408)
            while fd_jobs:
                j = fd_jobs.pop(0)
                cast(FD, FD[:, j, :], ffn_down[j * 128:(j + 1) * 128, :], 1024)
            kb.barrier()

        h2b = sb([128, 8, 512], BF16)
        ub = [sb([128, 514], F32) for _ in range(4)]
        uu = [sb([128, 512], F32) for _ in range(4)]
        sgt = [sb([128, 512], F32) for _ in range(2)]
        actT = sb([128, 22, 512], BF16)
        x1b = sb([128, 1024], F32)
        yt = sb([128, 1024], F32)
        blocks = [(0, 1)] + [(1 + 4 * k, 4) for k in range(4)]
        for (ti0, nt_) in blocks:
            if ti0 >= ntl:
                break
            nt_ = min(nt_, ntl - ti0)
            W = 128 * nt_
            halo = ti0 == 0
            kb.dma(h2b[:, :, 0:W], H2S[:, :, ti0 * 128:ti0 * 128 + W], writes=[h2b])
            for jj in range(22):
                us = []
                for k2, j in enumerate((jj, jj + 22)):
                    pb = bank()
                    for kc in range(8):
                        mm(pb, pb[:, 0:W], FU, FU[:, kc, j * 128:(j + 1) * 128], h2b, h2b[:, kc, 0:W], start=(kc == 0), stop=(kc == 7))
                    U = ub[(2 * jj + k2) % 4]
                    op("pool", lambda e: e.tensor_copy(out=U[:, 0:2], in_=HIST[:, j, :]), reads=[HIST], writes=[U])
                    op("act", lambda e: e.activation(out=U[:, 2:2 + W], in_=pb[:, 0:W], func=AF.Copy), reads=[pb], writes=[U])
                    op("pool", lambda e: e.tensor_copy(out=HIST[:, j, :], in_=U[:, W:W + 2]), reads=[U], writes=[HIST])
                    if halo:
                        continue
                    u = uu[(2 * jj + k2) % 4]
                    op("act", lambda e: e.activation(out=u[:, 0:W], in_=U[:, 2:2 + W], func=AF.Identity, bias=CB[:, j:j + 1], scale=CW[:, j, 2:3]),
                       reads=[U, CB, CW], writes=[u])
                    op("dve", lambda e: e.scalar_tensor_tensor(out=u[:, 0:W], in0=U[:, 1:1 + W], scalar=CW[:, j, 1:2], in1=u[:, 0:W], op0=ALU.mult, op1=ALU.add),
                       reads=[U, CW, u], writes=[u])
                    op("dve", lambda e: e.scalar_tensor_tensor(out=u[:, 0:W], in0=U[:, 0:W], scalar=CW[:, j, 0:1], in1=u[:, 0:W], op0=ALU.mult, op1=ALU.add),
                       reads=[U, CW, u], writes=[u])
                    us.append(u)
                if halo:
                    continue
                sgb = sgt[jj % 2]
                op("act", lambda e: e.activation(out=sgb[:, 0:W], in_=us[1][:, 0:W], func=AF.Silu), reads=[us[1]], writes=[sgb])
                op("pool", lambda e: e.tensor_tensor(out=actT[:, jj, 0:W], in0=sgb[:, 0:W], in1=us[0][:, 0:W], op=ALU.mult), reads=[sgb, us[0]], writes=[actT])
            if halo:
                op("dve", lambda e: e.tensor_scalar(out=HIST[:], in0=HIST[:], scalar1=HM[:, 0:1], scalar2=None, op0=ALU.mult),
                   reads=[HIST, HM], writes=[HIST])
                continue
            for tl in range(nt_):
                ti = ti0 + tl
                kb.dma(x1b[:], X1S[ti * 128:(ti + 1) * 128, :], writes=[x1b])
                for hf in range(2):
                    pb = bank()
                    for j in range(22):
                        mm(pb, pb[:], actT, actT[:, j, tl * 128:(tl + 1) * 128], FD, FD[:, j, hf * 512:(hf + 1) * 512], start=(j == 0), stop=(j == 21))
                    op("dve", lambda e: e.tensor_tensor(out=yt[:, hf * 512:(hf + 1) * 512], in0=pb[:], in1=x1b[:, hf * 512:(hf + 1) * 512], op=ALU.add),
                       reads=[pb, x1b], writes=[yt])
                kb.dma(y[(ti - 1) * 128:ti * 128, :], yt[:], reads=[yt], eng="pool")


def shared_inputs(inp):
    f = np.float32
    w_in = np.asarray(inp["w_in"][0], f)
    kc, vc = w_in[:, 512:640], w_in[:, 640:768]
    ksl, vsl = w_in[:, 768:896], w_in[:, 896:1024]
    kwn, vwn = w_in[:, 1024:1152], w_in[:, 1152:1280]
    d = {}
    d["g1"] = np.asarray(inp["norm1_g"], f).reshape(1, 1024)
    d["w_rw"] = np.ascontiguousarray(w_in[:, 1304:3096])
    d["mu"] = np.asarray(inp["rwkv_mu"], f).reshape(1, 1792)
    d["w_tm"] = np.ascontiguousarray(np.concatenate([vsl, vwn, ksl, kwn, kc, vc], 1))
    d["rwp"] = np.stack([np.asarray(inp[k], f).reshape(512) for k in
                         ("w0", "a0", "k_k", "k_a", "r_k", "ln_x_w", "ln_x_b")], 0)
    d["w2"] = np.asarray(inp["w2"][0], f)
    d["a2"] = np.asarray(inp["a2"][0], f)
    d["g2"] = np.asarray(inp["g2"][0], f)
    kg = np.asarray(inp["k_norm_g"][0], f)
    d["kngr"] = np.concatenate([kg[1], kg[1], kg[2], kg[2]]).reshape(1, 256)
    r = np.arange(128)
    d["tri"] = np.stack([(r[:, None] <= r[None, :]), (r[:, None] < r[None, :]), (r[:, None] > r[None, :])], 0).astype(f)
    d["idf"] = np.eye(128, dtype=f)
    d["bo"] = ((r[:, None] // 64) == (r[None, :] // 64)).astype(f)
    import ml_dtypes
    bf = ml_dtypes.bfloat16
    d["w_q"] = np.ascontiguousarray(w_in[:, 0:512])
    d["w_g"] = np.ascontiguousarray(w_in[:, 1280:1304])
    d["qng"] = np.tile(np.asarray(inp["q_norm_g"][0], f), 2).reshape(128, 1)
    d["kcg"] = np.tile(kg[0], 2).reshape(128, 1)
    w1 = np.asarray(inp["cmp_w1"][0], f).reshape(2, 32, 64, 128).transpose(0, 2, 1, 3)
    d["cw1"] = np.ascontiguousarray(np.concatenate([w1, w1], 1))
    ps = np.asarray(inp["cmp_pos"][0], f).transpose(0, 2, 1)
    d["cpos"] = np.ascontiguousarray(np.concatenate([ps, ps], 1))
    d["cb1"] = np.ascontiguousarray(np.asarray(inp["cmp_b1"][0], f).T)
    d["cw2"] = np.asarray(inp["cmp_w2"][0], f)
    b2 = np.asarray(inp["cmp_b2"][0], f)
    d["cb2k"] = np.tile(b2[0], 2).reshape(128, 1)
    d["cb2v"] = b2[1].reshape(1, 64)
    rb = np.asarray(inp["rel_bias"], f)
    n = np.arange(0, NT + 256)
    nf = np.maximum(n, 1).astype(f)
    large = 16 + (np.log(nf / f(16)) / f(math.log(8.0)) * f(16)).astype(np.int32)
    bk = np.where(n < 16, n, np.minimum(large, 31))
    q = np.arange(128)
    bmc = np.empty((18, 2, 128, 4, 128), f)
    for m in range(18):
        dd = (-31 + 128 * m) + q[None, :] - 16 * q[:, None]
        for gi in range(2):
            for h in range(4):
                bmc[m, gi, :, h, :] = np.where(dd >= 0, rb[bk[np.maximum(dd, 0)], 4 * gi + h], f(-1e30))
    d["bmc"] = bmc.reshape(18, 2, 128, 512)
    b31 = np.empty((2, 128, 4, 128), f)
    btd = np.empty((2, 128, 4, 128), f)
    bts = np.empty((2, 128, 4, 128), f)
    dq = q[None, :] - q[:, None]
    for gi in range(2):
        for h in range(4):
            b31[gi, :, h, :] = rb[31, 4 * gi + h]
            btd[gi, :, h, :] = rb[bk[np.maximum(dq, 0)], 4 * gi + h]
            bts[gi, :, h, :] = rb[bk[128 + dq], 4 * gi + h]
    d["b31t"] = b31.reshape(2, 128, 512)
    d["bt_diag"] = btd.reshape(2, 128, 512)
    d["bt_sub"] = bts.reshape(2, 128, 512)
    c = np.arange(512)[:, None]
    nn = np.arange(128)[None, :]
    ov = ((16 * c < 64 * nn + 64) & (16 * c + 31 >= 64 * nn) & (c < 511)).astype(f)
    ov1 = np.zeros((512, 132), f)
    ov1[:, 0:128] = ov
    ov1[:511, 128] = 1.0
    d["ovl"] = np.ascontiguousarray(ov1.reshape(4, 128, 132).transpose(1, 0, 2))
    d["bige"] = ((np.arange(NT)[None, :] // 64) == np.arange(128)[:, None]).astype(bf)
    gs = np.zeros((24, 24, 64), f)
    gs[np.arange(24), np.arange(24), :] = 1.0
    d["gsel"] = gs.reshape(24, 24 * 64).astype(bf)
    d["w_out"] = np.asarray(inp["w_out"][0], f)
    d["g2n"] = np.asarray(inp["norm2_g"], f).reshape(1, 1024)
    d["ffn_up"] = np.asarray(inp["ffn_up"][0], f)
    d["ffn_down"] = np.asarray(inp["ffn_down"][0], f)
    d["cw"] = np.ascontiguousarray(np.asarray(inp["conv_w"][0], f).T.reshape(44, 128, 3).transpose(1, 0, 2))
    d["cb"] = np.ascontiguousarray(np.asarray(inp["conv_b"][0], f).reshape(44, 128).T)
    return d


def core_inputs(inp, c):
    b, i = c // 4, c % 4
    pad = NT - NOWN * (i + 1)
    xp = np.zeros((NT, 1024), np.float32)
    xp[pad:] = inp["x"][b, :NOWN * (i + 1)]
    import ml_dtypes
    f = np.float32
    nb0 = pad // 64
    t = (47 + np.arange(17))[:, None, None] * 128 + np.arange(128)[None, :, None]
    n = np.arange(128)[None, None, :]
    valid = (n >= nb0) & (64 * n <= t) & (t >= pad)
    cur = t // 64
    forced = valid & ((n == nb0) | (n == cur) | (n == cur - 1))
    keep = (valid & ~forced).astype(f)
    add = np.where(forced, f(1e9), f(0.0)) + np.where(valid, f(0.0), f(-1e30))
    cc = np.arange(128)[:, None] + 128 * np.arange(4)[None, :]
    cvoid = np.where(16 * cc < pad, f(-1e30), f(0.0)).astype(f)
    onesv = np.zeros((128, 64, 64), f)
    onesv[:, pad // 128:, :] = 1.0
    return {"xp": xp, "hmask": np.full((128, 1), 0.0 if i == 0 else 1.0, f),
            "selkeep": keep, "seladd": add.astype(f), "selv": valid.astype(f), "cvoid": cvoid,
            "onesv": onesv.astype(ml_dtypes.bfloat16)}


def kernel(**inputs):
    inp = {k: np.asarray(v) for k, v in inputs.items()}
    nc = build()
    sh = shared_inputs(inp)
    in_maps = []
    for c in range(8):
        m = dict(sh)
        m.update(core_inputs(inp, c))
        in_maps.append(m)
    res = run_bass_kernel_spmd(nc, in_maps, core_ids=list(range(8)))
    out = np.empty((2, 8192, 1024), np.float32)
    for c in range(8):
        b, i = c // 4, c % 4
        out[b, NOWN * i:NOWN * (i + 1)] = res.results[c]["y"]
    return out
```

```python
import math
from contextlib import ExitStack

import numpy as np
import concourse.bass as bass
import concourse.mybir as mybir
from concourse.bass_utils import run_bass_kernel_spmd

F32 = mybir.dt.float32
BF16 = mybir.dt.bfloat16
AF = mybir.ActivationFunctionType
ALU = mybir.AluOpType
AX = mybir.AxisListType

NT = 8192
OWN0 = 6144
NOWN = 2048
C0 = math.exp(-0.5)
import os
P3LVL = int(os.environ.get('P3LVL', '9'))
P3SUB = int(os.environ.get('P3SUB', '9'))


class Buf:
    __slots__ = ("t", "w", "rs", "rd", "name", "psum")

    def __init__(self, t, name=""):
        self.t = t
        self.psum = False
        self.w = None
        self.rs = {}
        self.rd = []
        self.name = name

    def __getitem__(self, k):
        return self.t[k]


class Eng:
    def __init__(self, name, e, sem):
        self.name = name
        self.e = e
        self.sem = sem
        self.n = 0
        self.seen = {}


class DmaTok:
    def __init__(self, sem, val):
        self.sem = sem
        self.val = val


class KB:
    def __init__(self, nc, stack):
        self.nc = nc
        self.stack = stack
        self.engs = {}
        for name, e in (("pe", nc.tensor), ("act", nc.scalar), ("dve", nc.vector),
                        ("pool", nc.gpsimd), ("sp", nc.sync)):
            sem = stack.enter_context(nc.semaphore("s_" + name))
            self.engs[name] = Eng(name, e, sem)
        self.ND = 32
        self.dsem = [stack.enter_context(nc.semaphore("d%d" % i)) for i in range(self.ND)]
        self.dcount = [0] * self.ND
        self.di = 0
        self.nbuf = 0

    def sb(self, shape, dt, st=None):
        self.nbuf += 1
        name = "b%d" % self.nbuf
        return Buf((st or self.stack).enter_context(self.nc.sbuf_tensor(name, shape, dt)), name)

    def ps(self, shape, dt=F32):
        self.nbuf += 1
        name = "p%d" % self.nbuf
        b = Buf(self.stack.enter_context(self.nc.psum_tensor(name, shape, dt)), name)
        b.psum = True
        return b

    def _wait(self, X, dep):
        if dep is None:
            return
        src, n = dep
        if isinstance(src, DmaTok):
            key = (id(src.sem), src.val)
            if X.seen.get(key):
                return
            X.e.wait_ge(src.sem, src.val)
            X.seen[key] = 1
            return
        if src is X and X.name == "pe":
            return
        if X.seen.get(src.name, 0) >= n:
            return
        X.e.wait_ge(src.sem, n)
        X.seen[src.name] = n

    def _deps(self, X, reads, writes):
        for b in reads:
            self._wait(X, b.w)
            if b.psum:
                for name, n in b.rs.items():
                    if name != X.name:
                        self._wait(X, (self.engs[name], n))
        for b in writes:
            self._wait(X, b.w)
            for name, n in b.rs.items():
                self._wait(X, (self.engs[name], n))
            for t in b.rd:
                self._wait(X, (t, 0))

    def _mark(self, tok, reads, writes):
        for b in reads:
            if isinstance(tok[0], DmaTok):
                b.rd.append(tok[0])
            else:
                b.rs[tok[0].name] = tok[1]
        for b in writes:
            b.w = tok
            b.rs = {}
            b.rd = []

    def op(self, eng, fn, reads=(), writes=()):
        X = self.engs[eng]
        self._deps(X, reads, writes)
        ins = fn(X.e)
        X.n += 1
        ins.then_inc(X.sem, 1)
        self._mark((X, X.n), reads, writes)
        return ins

    def dma(self, out, in_, reads=(), writes=(), eng="sp"):
        X = self.engs[eng]
        i = self.di
        self.di = (self.di + 1) % self.ND
        if self.dcount[i] > 0:
            key = (id(self.dsem[i]), 16 * self.dcount[i])
            if not X.seen.get(key):
                X.e.wait_ge(self.dsem[i], 16 * self.dcount[i])
                X.seen[key] = 1
        self._deps(X, reads, writes)
        self.dcount[i] += 1
        tok = DmaTok(self.dsem[i], 16 * self.dcount[i])
        X.e.dma_start(out=out, in_=in_).then_inc(self.dsem[i], 16)
        self._mark((tok, 0), reads, writes)
        return tok

    def barrier(self):
        for X in self.engs.values():
            for i in range(self.ND):
                if self.dcount[i] > 0:
                    key = (id(self.dsem[i]), 16 * self.dcount[i])
                    if not X.seen.get(key):
                        X.e.wait_ge(self.dsem[i], 16 * self.dcount[i])
                        X.seen[key] = 1
            for name, E in self.engs.items():
                if E.n > 0 and X.seen.get(name, 0) < E.n:
                    X.e.wait_ge(E.sem, E.n)
                    X.seen[name] = E.n

    def finish(self):
        X = self.engs["sp"]
        for i in range(self.ND):
            if self.dcount[i] > 0:
                X.e.wait_ge(self.dsem[i], 16 * self.dcount[i])
        for name, E in self.engs.items():
            if name != "sp" and E.n > 0:
                X.e.wait_ge(E.sem, E.n)


def build(dbg=False, ntiles=64, phases=(1, 2, 3, 4), p4tiles=17, p3tiles=17):
    nc = bass.Bass("TRN2", target_bir_lowering=False)

    def din(name, shape, dt=F32):
        return nc.dram_tensor(name, shape, dt, kind="ExternalInput").ap()

    def dscr(name, shape, dt):
        return nc.dram_tensor(name, shape, dt, kind="Internal").ap()

    def dout(name, shape, dt=F32):
        return nc.dram_tensor(name, shape, dt, kind="ExternalOutput").ap()

    xp = din("xp", [NT, 1024])
    g1 = din("g1", [1, 1024])
    w_rw = din("w_rw", [1024, 1792])
    mu = din("mu", [1, 1792])
    w_tm = din("w_tm", [1024, 768])
    rwp = din("rwp", [7, 512])
    w2 = din("w2", [64, 512])
    a2 = din("a2", [64, 512])
    g2 = din("g2", [128, 512])
    kngr = din("kngr", [1, 256])
    tri = din("tri", [3, 128, 128])
    idf_d = din("idf", [128, 128])
    bo_d = din("bo", [128, 128])

    w_out = din("w_out", [1024, 1024])
    g2n = din("g2n", [1, 1024])
    ffn_up = din("ffn_up", [1024, 5632])
    ffn_down = din("ffn_down", [2816, 1024])
    cw = din("cw", [128, 44, 3])
    cb = din("cb", [128, 44])
    hmask = din("hmask", [128, 1])
    w_q = din("w_q", [1024, 512])
    w_g = din("w_g", [1024, 24])
    qng = din("qng", [128, 1])
    kcg = din("kcg", [128, 1])
    cw1 = din("cw1", [2, 128, 32, 128])
    cpos = din("cpos", [2, 128, 32])
    cb1 = din("cb1", [128, 2])
    cw2 = din("cw2", [2, 128, 64])
    cb2k = din("cb2k", [128, 1])
    cb2v = din("cb2v", [1, 64])
    bmc = din("bmc", [18, 2, 128, 512])
    b31t = din("b31t", [2, 128, 512])
    bt_diag = din("bt_diag", [2, 128, 512])
    bt_sub = din("bt_sub", [2, 128, 512])
    ovl = din("ovl", [128, 4, 132])
    bige = din("bige", [128, NT], BF16)
    gsel = din("gsel", [24, 24 * 64], BF16)
    selkeep = din("selkeep", [17, 128, 128])
    seladd = din("seladd", [17, 128, 128])
    selv = din("selv", [17, 128, 128])
    cvoid = din("cvoid", [128, 4])
    onesv = din("onesv", [128, 64, 64], BF16)
    KS2 = [dscr("ks2_%d" % g, [128, NT], BF16) for g in range(2)]
    KW2 = [dscr("kw2_%d" % g, [128, NT], BF16) for g in range(2)]
    KCT = dscr("kct", [128, NT], BF16)
    VCT = dscr("vct", [128, NT], BF16)
    VSW = dscr("vsw", [NT, 256], BF16)
    ORT = dscr("ort", [512, NOWN + 128], BF16)
    HTO = dscr("hto", [128, 8, NOWN + 128], BF16)

    if 3 in phases or 4 not in phases:
        ONT = dscr("ont", [512, NOWN + 128], BF16)
    else:
        ONT = din("ont", [512, NOWN + 128], BF16)
        ORT = din("ort_in", [512, NOWN + 128], BF16)
    X1S = dscr("x1s", [NOWN + 128, 1024], F32)
    H2S = dscr("h2s", [128, 8, NOWN + 128], BF16)
    y = dout("y", [NOWN, 1024])
    dbg_o = {}
    if dbg:
        dbg_o["orw"] = dout("d_orw", [NOWN + 128, 512])
        dbg_o["ks2"] = dout("d_ks2", [128, NT], BF16)
        dbg_o["vsw"] = dout("d_vsw", [NT, 256], BF16)
        dbg_o["ont"] = dout("d_ont", [512, NOWN + 128], BF16)
        dbg_o["kc"] = dout("d_kc", [2, 128, 512])
        dbg_o["vc"] = dout("d_vc", [2, 128, 4, 64])
        dbg_o["imp"] = dout("d_imp", [17, 2, 128, 128])

    with ExitStack() as st:
        kb = KB(nc, st)
        op = kb.op
        PT = kb.ps([128, 1024], BF16)
        banks = [kb.ps([128, 512], F32) for _ in range(7)]
        bank_i = [0]

        def bank():
            b = banks[bank_i[0] % 7]
            bank_i[0] += 1
            return b

        def mm(ob, oap, lb, lap, rb, rap, start=True, stop=True):
            op("pe", lambda e: e.matmul(oap, lhsT=lap, rhs=rap, start=start, stop=stop),
               reads=[lb, rb], writes=[ob])

        idf = kb.sb([128, 128], F32)
        idb = kb.sb([128, 128], BF16)
        TRI = [kb.sb([128, 128], F32) for _ in range(3)]
        kb.dma(idf[:], idf_d, writes=[idf])
        op("dve", lambda e: e.tensor_copy(out=idb[:], in_=idf[:]), reads=[idf], writes=[idb])
        for i in range(3):
            kb.dma(TRI[i][:], tri[i], writes=[TRI[i]])
        UI, US, LS = TRI
        BO = kb.sb([128, 128], F32)
        kb.dma(BO[:], bo_d, writes=[BO])
        ones_c = kb.sb([128, 1], F32)
        op("pool", lambda e: e.memset(ones_c[:], 1.0), writes=[ones_c])

        if 1 in phases:
            phase1(nc, kb, st, locals())
        if 3 in phases:
            kb.barrier()
            phase3(nc, kb, st, locals())
        if 4 in phases:
            kb.barrier()
            phase4(nc, kb, st, locals())
        kb.finish()
    return nc


def phase1(nc, kb, st0, G):
    op = kb.op
    mm = G["mm"]
    bank = G["bank"]
    PT = G["PT"]
    idf, idb, UI, US, LS, ones_c = G["idf"], G["idb"], G["UI"], G["US"], G["LS"], G["ones_c"]
    xp, g1, w_rw, mu, w_tm, rwp = G["xp"], G["g1"], G["w_rw"], G["mu"], G["w_tm"], G["rwp"]
    w2, a2, g2, kngr = G["w2"], G["a2"], G["g2"], G["kngr"]
    KS2, KW2, KCT, VCT, VSW, ORT, HTO = G["KS2"], G["KW2"], G["KCT"], G["VCT"], G["VSW"], G["ORT"], G["HTO"]
    dbg_o = G["dbg_o"]
    ntiles = G["ntiles"]
    with ExitStack() as st:
        def sb(shape, dt):
            return kb.sb(shape, dt, st)

        WR1 = sb([128, 8, 1792], BF16)
        WR2 = sb([128, 8, 1792], BF16)
        WTM = sb([128, 8, 768], BF16)
        G1B = sb([128, 1024], F32)
        PRM = [sb([128, 512], F32) for _ in range(7)]
        W0B, A0B, KKB, KAB, RKB, LNW, LNB = PRM
        w2b = sb([64, 512], BF16)
        a2b = sb([64, 512], BF16)
        g2b = sb([128, 512], BF16)
        KNG = sb([128, 256], F32)
        kb.dma(G1B[:], g1.partition_broadcast(128), writes=[G1B])
        for i in range(7):
            kb.dma(PRM[i][:], rwp[i:i + 1, :].partition_broadcast(128), writes=[PRM[i]])
        kb.dma(KNG[:], kngr.partition_broadcast(128), writes=[KNG])
        with ExitStack() as st_setup:
            def sbs(shape, dt):
                return kb.sb(shape, dt, st_setup)
            mub = sbs([128, 1792], F32)
            omm = sbs([128, 1792], F32)
            kb.dma(mub[:], mu.partition_broadcast(128), writes=[mub])
            op("dve", lambda e: e.tensor_scalar(out=omm[:], in0=mub[:], scalar1=-1.0, scalar2=1.0,
                                                op0=ALU.mult, op1=ALU.add), reads=[mub], writes=[omm])
            stg = [sbs([128, 1792], F32) for _ in range(2)]
            stg2 = [sbs([128, 1024], F32) for _ in range(2)]
            for kc in range(8):
                s = stg[kc % 2]
                kb.dma(s[:], w_rw[kc * 128:(kc + 1) * 128, :], writes=[s])
                op("dve", lambda e: e.tensor_tensor(out=WR1[:, kc, :], in0=s[:], in1=omm[:], op=ALU.mult),
                   reads=[s, omm], writes=[WR1])
                op("pool", lambda e: e.tensor_tensor(out=WR2[:, kc, :], in0=s[:], in1=mub[:], op=ALU.mult),
                   reads=[s, mub], writes=[WR2])
                s2 = stg2[kc % 2]
                kb.dma(s2[:, 0:768], w_tm[kc * 128:(kc + 1) * 128, :], writes=[s2])
                op("act", lambda e: e.activation(out=WTM[:, kc, :], in_=s2[:, 0:768], func=AF.Copy),
                   reads=[s2], writes=[WTM])
            for src, dst, p in ((w2, w2b, 64), (a2, a2b, 64), (g2, g2b, 128)):
                s = stg[0]
                kb.dma(s[0:p, 0:512], src, writes=[s])
                op("dve", lambda e: e.tensor_copy(out=dst[:], in_=s[0:p, 0:512]), reads=[s], writes=[dst])
            kb.barrier()

        xt = [sb([128, 1024], F32) for _ in range(2)]
        xfirst = [True]
        ss1 = sb([128, 1], F32)
        xn = sb([128, 1024], BF16)
        HT = [sb([128, 8, 129], BF16) for _ in range(2)]
        op("pool", lambda e: e.memset(HT[0][:, :, 0:1], 0.0), writes=[HT[0]])
        op("pool", lambda e: e.memset(HT[1][:, :, 0:1], 0.0), writes=[HT[1]])

        def f32t():
            return sb([128, 512], F32)

        Rs, Ks, Vs = f32t(), f32t(), f32t()
        Vb = [sb([128, 512], BF16) for _ in range(2)]
        thx = sb([64, 256], BF16)
        L16 = sb([128, 256], BF16)
        knT = sb([64, 512], BF16)
        kcT = sb([128, 256], BF16)
        k4 = sb([128, 4], F32)
        sgx = sb([128, 128], BF16)
        sg, av = f32t(), f32t()
        z1, za1 = sg, av
        kk, kkn, kp, bq = f32t(), f32t(), f32t(), f32t()
        tA = f32t()
        kk2 = t1 = tA
        s8 = sb([128, 8], F32)
        rn8 = sb([128, 8], F32)
        eL, eNL, eLex, eD = (sb([128, 512], BF16) for _ in range(4))
        rt = sb([128, 512], BF16)
        at = sb([128, 512], BF16)
        bt = sb([128, 512], BF16)
        kt = sb([128, 512], BF16)
        Bh = [sb([128, 512], BF16) for _ in range(2)]
        Kh = [sb([128, 512], BF16) for _ in range(2)]
        WC = [sb([64, 8], F32) for _ in range(2)]
        rtT = [sb([64, 1024], BF16) for _ in range(2)]
        atT = [sb([64, 1024], BF16) for _ in range(2)]
        btT = sb([64, 1024], BF16)
        ktT = sb([64, 1024], BF16)
        Nm = [sb([128, 512], BF16) for _ in range(2)]
        Mm = [sb([128, 512], BF16) for _ in range(2)]
        N2 = [sb([128, 512], BF16) for _ in range(2)]
        M2 = [sb([128, 512], BF16) for _ in range(2)]
        AKt = [[sb([128, 512], BF16) for _ in range(2)] for _ in range(2)]
        RBt = [[sb([128, 512], BF16) for _ in range(2)] for _ in range(2)]
        RKt = [[sb([128, 512], BF16) for _ in range(2)] for _ in range(2)]
        Pm = [[sb([128, 512], BF16) for _ in range(2)] for _ in range(2)]
        rk = tA
        rs8 = [sb([128, 8], F32) for _ in range(2)]
        Vkeep = [f32t() for _ in range(2)]
        Gkeep = [f32t() for _ in range(2)]
        ST = sb([64, 512], F32)
        STb = sb([64, 512], BF16)
        op("pool", lambda e: e.memset(ST[:], 0.0), writes=[ST])
        op("pool", lambda e: e.memset(STb[:], 0.0), writes=[STb])
        Gb = sb([128, 512], BF16)
        Ub = sb([128, 512], BF16)
        stmp = sb([64, 512], F32)
        ysb, ysq, yn = f32t(), f32t(), f32t()
        m8, v8, r8 = sb([128, 8], F32), sb([128, 8], F32), sb([128, 8], F32)
        ob16 = sb([128, 512], BF16)
        oT = sb([128, 512], BF16)
        ksq = tA
        krs = f32t()
        kn16 = sb([128, 512], BF16)
        kc16 = sb([128, 256], BF16)
        vsw16 = sb([128, 256], BF16)

        def b3(ap):
            return ap.rearrange("p (h d) -> p h d", h=8)

        def bc8(ap8, p=128):
            return ap8.unsqueeze(2).to_broadcast([p, 8, 64])

        def prep(tt, sgen=None):
            par = tt % 2
            own = tt >= OWN0 // 128 - 1
            x_t = xt[par]
            ht = HT[par]
            if xfirst[0]:
                xfirst[0] = False
                kb.dma(x_t[:], xp[tt * 128:(tt + 1) * 128, :], writes=[x_t])
            if tt + 1 < 64:
                kb.dma(xt[1 - par][:], xp[(tt + 1) * 128:(tt + 2) * 128, :], writes=[xt[1 - par]])
            op("act", lambda e: e.activation(out=xn[:], in_=x_t[:], func=AF.Square, accum_out=ss1[:]),
               reads=[x_t], writes=[xn, ss1])
            op("dve", lambda e: e.tensor_scalar(out=ss1[:], in0=ss1[:], scalar1=1.0 / 1024, scalar2=1e-6,
                                                op0=ALU.mult, op1=ALU.add), reads=[ss1], writes=[ss1])
            op("act", lambda e: e.activation(out=ss1[:], in_=ss1[:], func=AF.Sqrt), reads=[ss1], writes=[ss1])
            op("dve", lambda e: e.reciprocal(out=ss1[:], in_=ss1[:]), reads=[ss1], writes=[ss1])
            op("dve", lambda e: e.scalar_tensor_tensor(out=xn[:], in0=x_t[:], scalar=ss1[:, 0:1], in1=G1B[:],
                                                       op0=ALU.mult, op1=ALU.mult),
               reads=[x_t, ss1, G1B], writes=[xn])
            for kc in range(8):
                op("pe", lambda e: e.transpose(out=PT[:, kc * 128:(kc + 1) * 128],
                                               in_=xn[:, kc * 128:(kc + 1) * 128], identity=idb[:]),
                   reads=[xn, idb], writes=[PT])
            op("act", lambda e: e.activation(out=ht[:, :, 1:129], in_=PT[:].rearrange("p (k t) -> p k t", k=8),
                                             func=AF.Copy), reads=[PT], writes=[ht])
            op("pool", lambda e: e.tensor_copy(out=HT[1 - par][:, :, 0:1], in_=ht[:, :, 128:129]),
               reads=[ht], writes=[HT[1 - par]])
            if own:
                kb.dma(HTO[:, :, (tt - 47) * 128:(tt - 46) * 128], ht[:, :, 1:129], reads=[ht])

            def tm_proj(c0, dst, eng):
                pb = bank()
                for j in range(16):
                    kc = j % 8
                    if j < 8:
                        mm(pb, pb[:], ht, ht[:, kc, 1:129], WR1, WR1[:, kc, c0:c0 + 512], start=(j == 0), stop=False)
                    else:
                        mm(pb, pb[:], ht, ht[:, kc, 0:128], WR2, WR2[:, kc, c0:c0 + 512], start=False, stop=(j == 15))
                if eng == "act":
                    op("act", lambda e: e.activation(out=dst[:], in_=pb[:], func=AF.Copy), reads=[pb], writes=[dst])
                else:
                    op("dve", lambda e: e.tensor_copy(out=dst[:], in_=pb[:]), reads=[pb], writes=[dst])
            tm_proj(0, Rs, "act")
            tm_proj(512, Ks, "dve")
            tm_proj(1024, Vs, "act")
            op("pool", lambda e: e.tensor_copy(out=Vb[par][:], in_=Vs[:]), reads=[Vs], writes=[Vb[par]])
            pA = bank()
            for kc in range(8):
                mm(pA, pA[:], ht, ht[:, kc, 1:129], WTM, WTM[:, kc, 0:512], start=(kc == 0), stop=(kc == 7))
            pB = bank()
            for kc in range(8):
                mm(pB, pB[:, 0:256], ht, ht[:, kc, 1:129], WTM, WTM[:, kc, 512:768], start=(kc == 0), stop=(kc == 7))
            pC = bank()
            for j in range(16):
                kc = j % 8
                if j < 8:
                    mm(pC, pC[:, 0:256], ht, ht[:, kc, 1:129], WR1, WR1[:, kc, 1536:1792], start=(j == 0), stop=False)
                else:
                    mm(pC, pC[:, 0:256], ht, ht[:, kc, 0:128], WR2, WR2[:, kc, 1536:1792], start=False, stop=(j == 15))
            op("act", lambda e: e.activation(out=vsw16[:], in_=pA[:, 0:256], func=AF.Copy), reads=[pA], writes=[vsw16])
            kb.dma(VSW[tt * 128:(tt + 1) * 128, :], vsw16[:], reads=[vsw16])
            op("act", lambda e: e.activation(out=ksq[:, 0:256], in_=pA[:, 256:512], func=AF.Square), reads=[pA], writes=[ksq])
            op("dve", lambda e: e.tensor_reduce(out=k4[:], in_=ksq[:, 0:256].rearrange("p (h d) -> p h d", h=4), axis=AX.X, op=ALU.add),
               reads=[ksq], writes=[k4])
            op("dve", lambda e: e.tensor_scalar(out=k4[:], in0=k4[:], scalar1=1.0 / 64, scalar2=1e-6, op0=ALU.mult, op1=ALU.add), reads=[k4], writes=[k4])
            op("act", lambda e: e.activation(out=k4[:], in_=k4[:], func=AF.Sqrt), reads=[k4], writes=[k4])
            op("dve", lambda e: e.reciprocal(out=k4[:], in_=k4[:]), reads=[k4], writes=[k4])
            op("dve", lambda e: e.tensor_tensor(out=krs[:, 0:256].rearrange("p (h d) -> p h d", h=4),
                                                in0=pA[:, 256:512].rearrange("p (h d) -> p h d", h=4),
                                                in1=k4[:].unsqueeze(2).to_broadcast([128, 4, 64]), op=ALU.mult), reads=[pA, k4], writes=[krs])
            op("pool", lambda e: e.tensor_tensor(out=kn16[:, 0:256], in0=krs[:, 0:256], in1=KNG[:], op=ALU.mult), reads=[krs, KNG], writes=[kn16])
            op("act", lambda e: e.activation(out=kc16[:], in_=pB[:, 0:256], func=AF.Copy), reads=[pB], writes=[kc16])
            op("act", lambda e: e.activation(out=L16[:, 0:64], in_=pC[:, 0:64], func=AF.Tanh), reads=[pC], writes=[L16])
            op("act", lambda e: e.activation(out=L16[:, 64:128], in_=pC[:, 64:128], func=AF.Copy), reads=[pC], writes=[L16])
            op("act", lambda e: e.activation(out=L16[:, 128:256], in_=pC[:, 128:256], func=AF.Tanh, scale=0.5), reads=[pC], writes=[L16])
            op("pool", lambda e: e.tensor_scalar(out=L16[:, 128:256], in0=L16[:, 128:256], scalar1=0.5, scalar2=0.5, op0=ALU.mult, op1=ALU.add),
               reads=[L16], writes=[L16])
            for i in range(4):
                op("pe", lambda e: e.transpose(out=PT[0:64, i * 128:(i + 1) * 128], in_=kn16[:, i * 64:(i + 1) * 64], identity=idb[:]),
                   reads=[kn16, idb], writes=[PT])
            for i in range(2):
                op("pe", lambda e: e.transpose(out=PT[:, 512 + i * 128:512 + (i + 1) * 128], in_=kc16[:, i * 128:(i + 1) * 128], identity=idb[:]),
                   reads=[kc16, idb], writes=[PT])
            for i in range(2):
                op("pe", lambda e: e.transpose(out=PT[0:64, 768 + i * 128:768 + (i + 1) * 128], in_=L16[:, i * 64:(i + 1) * 64], identity=idb[:]),
                   reads=[L16, idb], writes=[PT])
            op("act", lambda e: e.activation(out=knT[:], in_=PT[0:64, 0:512], func=AF.Copy), reads=[PT], writes=[knT])
            op("dve", lambda e: e.tensor_copy(out=kcT[:], in_=PT[:, 512:768]), reads=[PT], writes=[kcT])
            op("act", lambda e: e.activation(out=thx[:], in_=PT[0:64, 768:1024], func=AF.Copy), reads=[PT], writes=[thx])
            op("pe", lambda e: e.transpose(out=PT[:, 0:128], in_=L16[:, 128:256], identity=idb[:]), reads=[L16, idb], writes=[PT])
            op("dve", lambda e: e.tensor_copy(out=sgx[:], in_=PT[:, 0:128]), reads=[PT], writes=[sgx])
            for g in range(2):
                kb.dma(KS2[g][0:64, tt * 128:(tt + 1) * 128], knT[:, g * 128:(g + 1) * 128], reads=[knT])
                kb.dma(KW2[g][0:64, tt * 128:(tt + 1) * 128], knT[:, 256 + g * 128:256 + (g + 1) * 128], reads=[knT])
            kb.dma(KCT[:, tt * 128:(tt + 1) * 128], kcT[:, 0:128], reads=[kcT])
            kb.dma(VCT[:, tt * 128:(tt + 1) * 128], kcT[:, 128:256], reads=[kcT])

            pz = bank()
            mm(pz, pz[:], thx, thx[:, 0:128], w2b, w2b[:])
            op("dve", lambda e: e.tensor_tensor(out=z1[:], in0=pz[:], in1=W0B[:], op=ALU.add), reads=[pz, W0B], writes=[z1])
            op("act", lambda e: e.activation(out=sg[:], in_=z1[:], func=AF.Tanh, scale=0.5), reads=[z1], writes=[sg])
            op("pool", lambda e: e.tensor_scalar(out=sg[:], in0=sg[:], scalar1=0.5, scalar2=0.5, op0=ALU.mult, op1=ALU.add),
               reads=[sg], writes=[sg])
            pa = bank()
            mm(pa, pa[:], thx, thx[:, 128:256], a2b, a2b[:])
            op("dve", lambda e: e.tensor_tensor(out=za1[:], in0=pa[:], in1=A0B[:], op=ALU.add), reads=[pa, A0B], writes=[za1])
            op("act", lambda e: e.activation(out=av[:], in_=za1[:], func=AF.Tanh, scale=0.5), reads=[za1], writes=[av])
            op("pool", lambda e: e.tensor_scalar(out=av[:], in0=av[:], scalar1=0.5, scalar2=0.5, op0=ALU.mult, op1=ALU.add),
               reads=[av], writes=[av])
            if own:
                pg = bank()
                mm(pg, pg[:], sgx, sgx[:], g2b, g2b[:])
                op("act", lambda e: e.activation(out=Gkeep[par][:], in_=pg[:], func=AF.Copy), reads=[pg], writes=[Gkeep[par]])
            op("pool", lambda e: e.tensor_tensor(out=kk[:], in0=Ks[:], in1=KKB[:], op=ALU.mult), reads=[Ks, KKB], writes=[kk])
            op("pool", lambda e: e.tensor_tensor(out=kk2[:], in0=kk[:], in1=kk[:], op=ALU.mult), reads=[kk], writes=[kk2])
            op("dve", lambda e: e.tensor_reduce(out=s8[:], in_=b3(kk2[:]), axis=AX.X, op=ALU.add), reads=[kk2], writes=[s8])
            op("act", lambda e: e.activation(out=s8[:], in_=s8[:], func=AF.Sqrt), reads=[s8], writes=[s8])
            op("dve", lambda e: e.tensor_scalar(out=s8[:], in0=s8[:], scalar1=1e-12, scalar2=None, op0=ALU.max), reads=[s8], writes=[s8])
            op("dve", lambda e: e.reciprocal(out=rn8[:], in_=s8[:]), reads=[s8], writes=[rn8])
            op("dve", lambda e: e.tensor_tensor(out=b3(kkn[:]), in0=b3(kk[:]), in1=bc8(rn8[:]), op=ALU.mult), reads=[kk, rn8], writes=[kkn])
            op("dve", lambda e: e.scalar_tensor_tensor(out=t1[:], in0=av[:], scalar=-1.0, in1=KAB[:], op0=ALU.add, op1=ALU.mult),
               reads=[av, KAB], writes=[t1])
            op("dve", lambda e: e.scalar_tensor_tensor(out=kp[:], in0=t1[:], scalar=1.0, in1=Ks[:], op0=ALU.add, op1=ALU.mult),
               reads=[t1, Ks], writes=[kp])
            op("dve", lambda e: e.tensor_tensor(out=bq[:], in0=kkn[:], in1=av[:], op=ALU.mult), reads=[kkn, av], writes=[bq])
            p1, p2, p3 = bank(), bank(), bank()
            mm(p1, p1[:], UI, UI[:], sg, sg[:])
            mm(p2, p2[:], US, US[:], sg, sg[:])
            mm(p3, p3[:], LS, LS[:], sg, sg[:])
            op("act", lambda e: e.activation(out=eL[:], in_=p1[:], func=AF.Exp, scale=-C0), reads=[p1], writes=[eL])
            op("act", lambda e: e.activation(out=eNL[:], in_=p1[:], func=AF.Exp, scale=C0), reads=[p1], writes=[eNL])
            op("act", lambda e: e.activation(out=eLex[:], in_=p2[:], func=AF.Exp, scale=-C0), reads=[p2], writes=[eLex])
            op("act", lambda e: e.activation(out=eD[:], in_=p3[:], func=AF.Exp, scale=-C0), reads=[p3], writes=[eD])
            pw = bank()
            for h in range(8):
                mm(pw, pw[0:64, h:h + 1], sg, sg[:, h * 64:(h + 1) * 64], ones_c, ones_c[:])
            op("act", lambda e: e.activation(out=WC[par][:], in_=pw[0:64, 0:8], func=AF.Exp, scale=-C0), reads=[pw], writes=[WC[par]])
            op("dve", lambda e: e.tensor_tensor(out=rt[:], in0=Rs[:], in1=eL[:], op=ALU.mult), reads=[Rs, eL], writes=[rt])
            op("dve", lambda e: e.scalar_tensor_tensor(out=at[:], in0=kkn[:], scalar=-1.0, in1=eLex[:], op0=ALU.mult, op1=ALU.mult),
               reads=[kkn, eLex], writes=[at])
            op("dve", lambda e: e.tensor_tensor(out=bt[:], in0=bq[:], in1=eNL[:], op=ALU.mult), reads=[bq, eNL], writes=[bt])
            op("pool", lambda e: e.tensor_tensor(out=kt[:], in0=kp[:], in1=eNL[:], op=ALU.mult), reads=[kp, eNL], writes=[kt])
            op("dve", lambda e: e.tensor_tensor(out=Bh[par][:], in0=bq[:], in1=eD[:], op=ALU.mult), reads=[bq, eD], writes=[Bh[par]])
            op("pool", lambda e: e.tensor_tensor(out=Kh[par][:], in0=kp[:], in1=eD[:], op=ALU.mult), reads=[kp, eD], writes=[Kh[par]])
            if own:
                op("pool", lambda e: e.tensor_tensor(out=rk[:], in0=Rs[:], in1=kp[:], op=ALU.mult), reads=[Rs, kp], writes=[rk])
                op("pool", lambda e: e.tensor_tensor(out=rk[:], in0=rk[:], in1=RKB[:], op=ALU.mult), reads=[rk, RKB], writes=[rk])
                op("dve", lambda e: e.tensor_reduce(out=rs8[par][:], in_=b3(rk[:]), axis=AX.X, op=ALU.add), reads=[rk], writes=[rs8[par]])
                op("pool", lambda e: e.tensor_copy(out=Vkeep[par][:], in_=Vs[:]), reads=[Vs], writes=[Vkeep[par]])
            for ((srcA, dstA), (srcB, dstB)) in (((rt, rtT[par]), (at, atT[par])), ((bt, btT), (kt, ktT))):
                for q_, src in enumerate((srcA, srcB)):
                    for pr in range(4):
                        op("pe", lambda e: e.transpose(out=PT[:, (q_ * 4 + pr) * 128:(q_ * 4 + pr + 1) * 128], in_=src[:, pr * 128:(pr + 1) * 128],
                                                       identity=idb[:]), reads=[src, idb], writes=[PT])
                for q_, dst in enumerate((dstA, dstB)):
                    dv = dst[:].rearrange("p (a b t) -> p a b t", a=4, b=2)
                    op("act", lambda e: e.activation(out=dv[:, :, 0, :], in_=PT[0:64, q_ * 512:(q_ + 1) * 512].rearrange("p (a t) -> p a t", a=4),
                                                     func=AF.Copy), reads=[PT], writes=[dst])
                    op("dve", lambda e: e.tensor_copy(out=dv[:, :, 1, :], in_=PT[64:128, q_ * 512:(q_ + 1) * 512].rearrange("p (a t) -> p a t", a=4)),
                       reads=[PT], writes=[dst])
            def cmat(lT, rT, mask, dsts):
                for grp in range(2):
                    pb2 = bank()
                    for hh in range(4):
                        h = grp * 4 + hh
                        mm(pb2, pb2[:, hh * 128:(hh + 1) * 128], lT, lT[:, h * 128:(h + 1) * 128], rT, rT[:, h * 128:(h + 1) * 128])
                    d = dsts[grp]
                    op("dve", lambda e: e.tensor_tensor(out=d[:].rearrange("p (h t) -> p h t", h=4),
                                                        in0=pb2[:].rearrange("p (h t) -> p h t", h=4),
                                                        in1=mask[:].unsqueeze(1).to_broadcast([128, 4, 128]), op=ALU.mult),
                       reads=[pb2, mask], writes=[d])
            cmat(atT[par], btT, LS, Nm)
            cmat(btT, atT[par], US, Mm)
            cmat(ktT, atT[par], US, AKt[par])
            if own:
                cmat(btT, rtT[par], UI, RBt[par])
                cmat(ktT, rtT[par], UI, RKt[par])
            for grp in range(2):
                P = Pm[par][grp]
                op("dve", lambda e: e.tensor_tensor(out=P[:].rearrange("p (h t) -> p h t", h=4),
                                                    in0=Mm[grp][:].rearrange("p (h t) -> p h t", h=4),
                                                    in1=idf[:].unsqueeze(1).to_broadcast([128, 4, 128]), op=ALU.add),
                   reads=[Mm[grp], idf], writes=[P])
            cn, cm = Nm, Mm
            nn, nm = N2, M2
            for lvl in range(6):
                last = lvl == 5
                for grp in range(2):
                    pn = bank()
                    for hh in range(4):
                        sl = slice(hh * 128, (hh + 1) * 128)
                        mm(pn, pn[:, sl], cm[grp], cm[grp][:, sl], cn[grp], cn[grp][:, sl])
                    op("act", lambda e: e.activation(out=nn[grp][:], in_=pn[:], func=AF.Copy), reads=[pn], writes=[nn[grp]])
                    if not last:
                        pm_ = bank()
                        for hh in range(4):
                            sl = slice(hh * 128, (hh + 1) * 128)
                            mm(pm_, pm_[:, sl], cn[grp], cn[grp][:, sl], cm[grp], cm[grp][:, sl])
                        op("act", lambda e: e.activation(out=nm[grp][:], in_=pm_[:], func=AF.Copy), reads=[pm_], writes=[nm[grp]])
                    P = Pm[par][grp]
                    pp = bank()
                    for hh in range(4):
                        sl = slice(hh * 128, (hh + 1) * 128)
                        mm(pp, pp[:, sl], nn[grp], nn[grp][:, sl], P, P[:, sl])
                    op("dve", lambda e: e.tensor_tensor(out=P[:], in0=pp[:], in1=P[:], op=ALU.add), reads=[pp, P], writes=[P])
                cn, cm, nn, nm = nn, nm, cn, cm
                if sgen is not None:
                    next(sgen, None)

        def seq(tt):
            par = tt % 2
            own = tt >= OWN0 // 128 - 1
            pg = bank()
            for h in range(8):
                grp, hh = h // 4, h % 4
                hs = slice(h * 64, (h + 1) * 64)
                mm(pg, pg[:, hs], atT[par], atT[par][:, h * 128:(h + 1) * 128], STb, STb[:, hs], start=True, stop=False)
                mm(pg, pg[:, hs], AKt[par][grp], AKt[par][grp][:, hh * 128:(hh + 1) * 128], Vb[par], Vb[par][:, hs], start=False, stop=True)
            op("act", lambda e: e.activation(out=Gb[:], in_=pg[:], func=AF.Copy), reads=[pg], writes=[Gb])
            yield
            pu = bank()
            for h in range(8):
                grp, hh = h // 4, h % 4
                hs = slice(h * 64, (h + 1) * 64)
                mm(pu, pu[:, hs], Pm[par][grp], Pm[par][grp][:, hh * 128:(hh + 1) * 128], Gb, Gb[:, hs])
            op("dve", lambda e: e.tensor_copy(out=Ub[:], in_=pu[:]), reads=[pu], writes=[Ub])
            yield
            if own:
                py = bank()
                for h in range(8):
                    grp, hh = h // 4, h % 4
                    hs = slice(h * 64, (h + 1) * 64)
                    mm(py, py[:, hs], rtT[par], rtT[par][:, h * 128:(h + 1) * 128], STb, STb[:, hs], start=True, stop=False)
                    mm(py, py[:, hs], RBt[par][grp], RBt[par][grp][:, hh * 128:(hh + 1) * 128], Ub, Ub[:, hs], start=False, stop=False)
                    mm(py, py[:, hs], RKt[par][grp], RKt[par][grp][:, hh * 128:(hh + 1) * 128], Vb[par], Vb[par][:, hs], start=False, stop=True)
                op("act", lambda e: e.activation(out=ysb[:], in_=py[:], func=AF.Copy), reads=[py], writes=[ysb])
            yield
            pn = bank()
            for h in range(8):
                hs = slice(h * 64, (h + 1) * 64)
                mm(pn, pn[0:64, hs], Bh[par], Bh[par][:, hs], Ub, Ub[:, hs], start=True, stop=False)
                mm(pn, pn[0:64, hs], Kh[par], Kh[par][:, hs], Vb[par], Vb[par][:, hs], start=False, stop=True)
            op("pool", lambda e: e.tensor_tensor(out=b3(stmp[:]), in0=b3(ST[:]), in1=bc8(WC[par][:], 64), op=ALU.mult),
               reads=[ST, WC[par]], writes=[stmp])
            op("dve", lambda e: e.tensor_tensor(out=ST[:], in0=pn[0:64, :], in1=stmp[:], op=ALU.add), reads=[pn, stmp], writes=[ST])
            op("act", lambda e: e.activation(out=STb[:], in_=ST[:], func=AF.Copy), reads=[ST], writes=[STb])
            yield
            if own:
                op("dve", lambda e: e.tensor_reduce(out=m8[:], in_=b3(ysb[:]), axis=AX.X, op=ALU.add), reads=[ysb], writes=[m8])
                op("pool", lambda e: e.tensor_tensor(out=ysq[:], in0=ysb[:], in1=ysb[:], op=ALU.mult), reads=[ysb], writes=[ysq])
                op("dve", lambda e: e.tensor_reduce(out=v8[:], in_=b3(ysq[:]), axis=AX.X, op=ALU.add), reads=[ysq], writes=[v8])
                op("dve", lambda e: e.tensor_scalar(out=m8[:], in0=m8[:], scalar1=1.0 / 64, scalar2=None, op0=ALU.mult), reads=[m8], writes=[m8])
                op("dve", lambda e: e.tensor_tensor(out=r8[:], in0=m8[:], in1=m8[:], op=ALU.mult), reads=[m8], writes=[r8])
                op("dve", lambda e: e.scalar_tensor_tensor(out=v8[:], in0=v8[:], scalar=1.0 / 64, in1=r8[:], op0=ALU.mult, op1=ALU.subtract),
                   reads=[v8, r8], writes=[v8])
                op("dve", lambda e: e.tensor_scalar(out=v8[:], in0=v8[:], scalar1=64e-5, scalar2=None, op0=ALU.add), reads=[v8], writes=[v8])
                op("act", lambda e: e.activation(out=v8[:], in_=v8[:], func=AF.Sqrt), reads=[v8], writes=[v8])
                op("dve", lambda e: e.reciprocal(out=r8[:], in_=v8[:]), reads=[v8], writes=[r8])
                op("dve", lambda e: e.tensor_tensor(out=b3(yn[:]), in0=b3(ysb[:]), in1=bc8(m8[:]), op=ALU.subtract), reads=[ysb, m8], writes=[yn])
                op("dve", lambda e: e.tensor_tensor(out=b3(yn[:]), in0=b3(yn[:]), in1=bc8(r8[:]), op=ALU.mult), reads=[yn, r8], writes=[yn])
                op("pool", lambda e: e.tensor_tensor(out=yn[:], in0=yn[:], in1=LNW[:], op=ALU.mult), reads=[yn, LNW], writes=[yn])
                op("pool", lambda e: e.tensor_tensor(out=yn[:], in0=yn[:], in1=LNB[:], op=ALU.add), reads=[yn, LNB], writes=[yn])
                op("dve", lambda e: e.tensor_tensor(out=b3(ysq[:]), in0=b3(Vkeep[par][:]), in1=bc8(rs8[par][:]), op=ALU.mult),
                   reads=[Vkeep[par], rs8[par]], writes=[ysq])
                op("pool", lambda e: e.tensor_tensor(out=yn[:], in0=yn[:], in1=ysq[:], op=ALU.add), reads=[yn, ysq], writes=[yn])
                if dbg_o:
                    op("pool", lambda e: e.tensor_tensor(out=ysq[:], in0=yn[:], in1=Gkeep[par][:], op=ALU.mult), reads=[yn, Gkeep[par]], writes=[ysq])
                    kb.dma(dbg_o["orw"][(tt - 47) * 128:(tt - 46) * 128, :], ysq[:], reads=[ysq])
                op("pool", lambda e: e.tensor_tensor(out=ob16[:], in0=yn[:], in1=Gkeep[par][:], op=ALU.mult), reads=[yn, Gkeep[par]], writes=[ob16])
                for c in range(4):
                    op("pe", lambda e: e.transpose(out=PT[:, c * 128:(c + 1) * 128], in_=ob16[:, c * 128:(c + 1) * 128], identity=idb[:]),
                       reads=[ob16, idb], writes=[PT])
                op("act", lambda e: e.activation(out=oT[:], in_=PT[:, 0:512], func=AF.Copy), reads=[PT], writes=[oT])
                for c in range(4):
                    kb.dma(ORT[c * 128:(c + 1) * 128, (tt - 47) * 128:(tt - 46) * 128], oT[:, c * 128:(c + 1) * 128], reads=[oT])

        t0 = 64 - ntiles
        for tt in range(t0, 64):
            sgen = seq(tt - 1) if tt > t0 else None
            prep(tt, sgen)
            if sgen is not None:
                for _ in sgen:
                    pass
        for _ in seq(63):
            pass
        if dbg_o:
            kb.dma(dbg_o["ks2"], KS2[0], eng="sp")
            kb.dma(dbg_o["vsw"], VSW, eng="sp")


def phase3(nc, kb, st0, G):
    op = kb.op
    mm = G["mm"]
    PT = G["PT"]
    banks = G["banks"]
    idf, idb, UI, US, LS, BO = G["idf"], G["idb"], G["UI"], G["US"], G["LS"], G["BO"]
    dbg_o = G["dbg_o"]
    KS2, KW2, KCT, VCT, VSW, HTO, ONT = G["KS2"], G["KW2"], G["KCT"], G["VCT"], G["VSW"], G["HTO"], G["ONT"]
    g = lambda k: G[k]
    rot = banks[0:3]
    ACC0, ACC1 = banks[5], banks[6]
    MES = [banks[3], banks[4], banks[6]]
    ri = [0]
    mi = [0]

    def bank():
        b = rot[ri[0] % 3]
        ri[0] += 1
        return b

    with ExitStack() as st:
        def sb(shape, dt):
            return kb.sb(shape, dt, st)
        KSs = [sb([128, NT], BF16) for _ in range(2)]
        VS1 = [sb([128, 64, 128], BF16) for _ in range(2)]
        KWs = [sb([128, 22 * 128], BF16) for _ in range(2)]
        VW1 = [sb([128, 22, 128], BF16) for _ in range(2)]
        BIGE = sb([128, NT], BF16)
        for gi in range(2):
            kb.dma(KSs[gi][0:64, :], KS2[gi][0:64, :], writes=[KSs[gi]])
            kb.dma(KWs[gi][0:64, :], KW2[gi][0:64, 42 * 128:64 * 128], writes=[KWs[gi]])
        for gi in range(2):
            for j0 in range(0, 64, 8):
                kb.dma(VS1[gi][:, j0:j0 + 8, 0:64], VSW[j0 * 128:(j0 + 8) * 128, gi * 64:(gi + 1) * 64].rearrange("(j p) d -> p j d", p=128), writes=[VS1[gi]])
            for j0 in range(0, 64, 16):
                kb.dma(VS1[gi][:, j0:j0 + 16, 64:128], g("onesv")[:, j0:j0 + 16, :], writes=[VS1[gi]])
            for j0 in range(0, 22, 11):
                kb.dma(VW1[gi][:, j0:j0 + 11, 0:64], VSW[(42 + j0) * 128:(53 + j0) * 128, 128 + gi * 64:128 + (gi + 1) * 64].rearrange("(j p) d -> p j d", p=128), writes=[VW1[gi]])
            kb.dma(VW1[gi][:, :, 64:128], g("onesv")[:, 42:64, :], writes=[VW1[gi]])
        kb.dma(BIGE[:], g("bige"), writes=[BIGE])
        KCf = [sb([128, 512], F32) for _ in range(2)]
        VCf = [sb([128, 4, 128], F32) for _ in range(2)]
        for gi in range(2):
            op("pool", lambda e: e.memset(VCf[gi][:, :, 64:128], 1.0), writes=[VCf[gi]])
        OVL = sb([128, 4, 132], F32)
        kb.dma(OVL[:], g("ovl"), writes=[OVL])
        CVO = sb([128, 4], F32)
        kb.dma(CVO[:], g("cvoid"), writes=[CVO])
        B31 = [sb([128, 512], F32) for _ in range(2)]
        EMD = [sb([128, 512], BF16) for _ in range(2)]
        EMS = [sb([128, 512], BF16) for _ in range(2)]
        ones_f = sb([128, 128], F32)
        op("pool", lambda e: e.memset(ones_f[:], 1.0), writes=[ones_f])
        UIb = sb([128, 128], BF16)
        LSb = sb([128, 128], BF16)
        op("dve", lambda e: e.tensor_copy(out=UIb[:], in_=UI[:]), reads=[UI], writes=[UIb])
        op("dve", lambda e: e.tensor_copy(out=LSb[:], in_=LS[:]), reads=[LS], writes=[LSb])
        tmpf = sb([128, 512], F32)
        for gi in range(2):
            kb.dma(B31[gi][:], g("b31t")[gi], writes=[B31[gi]])
            for src, dst, msk in ((g("bt_diag"), EMD, UI), (g("bt_sub"), EMS, None)):
                kb.dma(tmpf[:], src[gi], writes=[tmpf])
                op("dve", lambda e: e.tensor_tensor(out=tmpf[:], in0=tmpf[:], in1=B31[gi][:], op=ALU.subtract), reads=[tmpf, B31[gi]], writes=[tmpf])
                op("act", lambda e: e.activation(out=tmpf[:], in_=tmpf[:], func=AF.Exp), reads=[tmpf], writes=[tmpf])
                if msk is not None:
                    op("dve", lambda e: e.tensor_tensor(out=dst[gi][:].rearrange("p (h q) -> p h q", h=4), in0=tmpf[:].rearrange("p (h q) -> p h q", h=4),
                                                        in1=msk[:].unsqueeze(1).to_broadcast([128, 4, 128]), op=ALU.mult), reads=[tmpf, msk], writes=[dst[gi]])
                else:
                    op("dve", lambda e: e.tensor_copy(out=dst[gi][:], in_=tmpf[:]), reads=[tmpf], writes=[dst[gi]])
        WQ = sb([128, 8, 512], BF16)
        WG = sb([128, 8, 24], BF16)
        GSEL = sb([24, 24 * 64], BF16)
        kb.dma(GSEL[:], g("gsel"), writes=[GSEL])
        QNG = sb([128, 1], F32)
        kb.dma(QNG[:], g("qng"), writes=[QNG])
        op("dve", lambda e: e.tensor_scalar(out=QNG[:], in0=QNG[:], scalar1=0.125, scalar2=None, op0=ALU.mult), reads=[QNG], writes=[QNG])
        for kc in range(8):
            kb.dma(tmpf[:], g("w_q")[kc * 128:(kc + 1) * 128, :], writes=[tmpf])
            op("dve", lambda e: e.tensor_copy(out=WQ[:, kc, :], in_=tmpf[:]), reads=[tmpf], writes=[WQ])
            kb.dma(tmpf[:, 0:24], g("w_g")[kc * 128:(kc + 1) * 128, :], writes=[tmpf])
            op("dve", lambda e: e.tensor_copy(out=WG[:, kc, :], in_=tmpf[:, 0:24]), reads=[tmpf], writes=[WG])

        with ExitStack() as st2:
            def sb2(shape, dt):
                return kb.sb(shape, dt, st2)
            kvb = [sb2([64, NT], BF16) for _ in range(2)]
            kvT = [kvb, kvb]
            W1 = sb2([128, 32, 128], BF16)
            W1f = sb2([128, 32, 128], F32)
            POS = sb2([128, 32], F32)
            W2f = sb2([128, 64], F32)
            W2d = sb2([128, 128], F32)
            CB1 = sb2([128, 2], F32)
            kb.dma(CB1[:], g("cb1"), writes=[CB1])
            CB2K = sb2([128, 1], F32)
            kb.dma(CB2K[:], g("cb2k"), writes=[CB2K])
            CB2V = sb2([128, 64], F32)
            kb.dma(CB2V[:], g("cb2v").partition_broadcast(128), writes=[CB2V])
            KCG = sb2([128, 1], F32)
            kb.dma(KCG[:], g("kcg"), writes=[KCG])
            cst = sb2([128, 1], F32)
            xh = sb2([128, 512], F32)
            x2 = sb2([128, 512], F32)
            ge = sb2([128, 512], F32)
            for kv in range(2):
                for gi in range(2):
                    kb.dma(kvb[gi][:], (KCT if kv == 0 else VCT)[gi * 64:(gi + 1) * 64, :], writes=[kvb[gi]])
                kb.dma(W1f[:], g("cw1")[kv], writes=[W1f])
                op("dve", lambda e: e.tensor_copy(out=W1[:], in_=W1f[:]), reads=[W1f], writes=[W1])
                kb.dma(POS[:], g("cpos")[kv], writes=[POS])
                kb.dma(W2f[:], g("cw2")[kv], writes=[W2f])
                op("dve", lambda e: e.tensor_copy(out=W2d[:, 0:64], in_=W2f[:]), reads=[W2f], writes=[W2d])
                op("dve", lambda e: e.tensor_copy(out=W2d[:, 64:128], in_=W2f[:]), reads=[W2f], writes=[W2d])
                pc = bank()
                for l in range(32):
                    mm(pc, pc[:, 0:1], W1f, W1f[0:64, l, :], POS, POS[0:64, l:l + 1], start=(l == 0), stop=(l == 31))
                op("dve", lambda e: e.tensor_tensor(out=cst[:], in0=pc[:, 0:1], in1=CB1[:, kv:kv + 1], op=ALU.add), reads=[pc, CB1], writes=[cst])
                for gi in range(2):
                    ph = bank()
                    for l in range(32):
                        mm(ph, ph[:, 0:511], W1, W1[0:64, l, :], kvT[kv][gi], kvT[kv][gi][:, l:l + 16 * 510 + 1:16],
                           start=(l == 0), stop=(l == 31))
                    op("pool", lambda e: e.memset(xh[:, 511:512], 0.0), writes=[xh])
                    op("act", lambda e: e.activation(out=xh[:, 0:511], in_=ph[:, 0:511], func=AF.Identity, bias=cst[:, 0:1]), reads=[ph, cst], writes=[xh])
                    op("pool", lambda e: e.tensor_tensor(out=x2[:], in0=xh[:], in1=xh[:], op=ALU.mult), reads=[xh], writes=[x2])
                    op("dve", lambda e: e.tensor_scalar(out=x2[:], in0=x2[:], scalar1=0.044715, scalar2=1.0, op0=ALU.mult, op1=ALU.add), reads=[x2], writes=[x2])
                    op("pool", lambda e: e.tensor_tensor(out=x2[:], in0=x2[:], in1=xh[:], op=ALU.mult), reads=[x2, xh], writes=[x2])
                    op("act", lambda e: e.activation(out=x2[:], in_=x2[:], func=AF.Tanh, scale=0.7978845608028654), reads=[x2], writes=[x2])
                    op("dve", lambda e: e.scalar_tensor_tensor(out=ge[:], in0=x2[:], scalar=1.0, in1=xh[:], op0=ALU.add, op1=ALU.mult), reads=[x2, xh], writes=[ge])
                    op("dve", lambda e: e.tensor_scalar(out=ge[:], in0=ge[:], scalar1=0.5, scalar2=None, op0=ALU.mult), reads=[ge], writes=[ge])
                    if kv == 0:
                        po = bank()
                        mm(po, po[:], W2d, W2d[:], ge, ge[:])
                        op("act", lambda e: e.activation(out=xh[:], in_=po[:], func=AF.Identity, bias=CB2K[:, 0:1]), reads=[po, CB2K], writes=[xh])
                        op("pool", lambda e: e.tensor_tensor(out=x2[:], in0=xh[:], in1=xh[:], op=ALU.mult), reads=[xh], writes=[x2])
                        pq = bank()
                        mm(pq, pq[:], BO, BO[:], x2, x2[:])
                        op("dve", lambda e: e.tensor_scalar(out=x2[:], in0=pq[:], scalar1=1.0 / 64, scalar2=1e-6, op0=ALU.mult, op1=ALU.add), reads=[pq], writes=[x2])
                        op("act", lambda e: e.activation(out=x2[:], in_=x2[:], func=AF.Sqrt), reads=[x2], writes=[x2])
                        op("dve", lambda e: e.reciprocal(out=x2[:], in_=x2[:]), reads=[x2], writes=[x2])
                        op("dve", lambda e: e.scalar_tensor_tensor(out=KCf[gi][:], in0=xh[:], scalar=KCG[:, 0:1], in1=x2[:], op0=ALU.mult, op1=ALU.mult),
                           reads=[xh, KCG, x2], writes=[KCf[gi]])
                        if dbg_o:
                            kb.dma(dbg_o["kc"][gi], KCf[gi][:], reads=[KCf[gi]])
                    else:
                        for ct in range(4):
                            po = bank()
                            mm(po, po[:, 0:64], ge, ge[:, ct * 128:(ct + 1) * 128], W2f, W2f[:])
                            op("dve", lambda e: e.tensor_tensor(out=VCf[gi][:, ct, 0:64], in0=po[:, 0:64], in1=CB2V[:], op=ALU.add), reads=[po, CB2V], writes=[VCf[gi]])
                        if dbg_o:
                            kb.dma(dbg_o["vc"][gi], VCf[gi][:, :, 0:64], reads=[VCf[gi]])
            kb.barrier()

        hto = sb([128, 8, 128], BF16)
        qsq = sb([128, 512], F32)
        qrs = sb([128, 512], F32)
        QNf = sb([128, 512], F32)
        QNb = sb([128, 512], BF16)
        gT = sb([24, 128], BF16)
        GBs = [sb([64, 3, 512], F32) for _ in range(2)]
        gbi = [0]
        PC = [sb([128, 512], F32) for _ in range(4)]
        bmt = [sb([128, 512], F32) for _ in range(2)]
        rden = sb([128, 512], F32)
        sk = sb([128, 128], F32)
        sa = sb([128, 128], F32)
        sv = sb([128, 128], F32)
        sc = sb([128, 128], F32)
        sc2 = sb([128, 128], F32)
        m8a = sb([128, 8], F32)
        m8b = sb([128, 8], F32)
        smb = sb([128, 128], BF16)
        mT = sb([128, 128], BF16)
        Eb = [sb([128, 512], BF16) for _ in range(4)]
        Pb = [sb([128, 512], BF16) for _ in range(4)]
        oacc = sb([64, 512], F32)
        otmp = sb([64, 512], F32)
        rd64 = sb([64, 512], F32)
        accs = [[sb([128, 512], F32) for _ in range(3)] for _ in range(2)]
        acs = [0]
        epg = [None]
        rdq = sb([128, 4], F32)
        o16 = sb([64, 512], BF16)
        ei = [0]

        def qhalf(r):
            return slice((r % 2) * 64, (r % 2) * 64 + 64), slice((r // 2) * 128, (r // 2) * 128 + 128)

        def finish_branch(br, first):
            op("act", lambda e: e.activation(out=accs[acs[0] % 2][br][:], in_=ACC0[:], func=AF.Copy), reads=[ACC0], writes=[accs[acs[0] % 2][br]])

        def epilogue(GBc, acc3, qi_, gi_):
            for br in range(3):
                A = acc3[br]
                op("dve", lambda e: e.tensor_scalar(out=rd64[:], in0=A[64:128, :], scalar1=1e-30, scalar2=None, op0=ALU.max), reads=[A], writes=[rd64])
                yield
                op("dve", lambda e: e.reciprocal(out=rd64[:], in_=rd64[:]), reads=[rd64], writes=[rd64])
                yield
                op("pool", lambda e: e.tensor_tensor(out=rd64[:], in0=rd64[:], in1=GBc[:, br, :], op=ALU.mult), reads=[rd64, GBc], writes=[rd64])
                yield
                if br == 0:
                    op("pool", lambda e: e.tensor_tensor(out=oacc[:], in0=A[0:64, :], in1=rd64[:], op=ALU.mult), reads=[A, rd64], writes=[oacc])
                    yield
                else:
                    op("pool", lambda e: e.tensor_tensor(out=otmp[:], in0=A[0:64, :], in1=rd64[:], op=ALU.mult), reads=[A, rd64], writes=[otmp])
                    yield
                    dst_ = o16 if br == 2 else oacc
                    op("pool", lambda e: e.tensor_tensor(out=dst_[:], in0=oacc[:], in1=otmp[:], op=ALU.add), reads=[oacc, otmp], writes=[dst_])
                    yield
            kb.dma(ONT[gi_ * 256:(gi_ + 1) * 256, qi_ * 128:(qi_ + 1) * 128].rearrange("(r d) q -> d r q", d=64),
                   o16[:].rearrange("d (r q) -> d r q", r=4), reads=[o16], eng="pool")
            yield

        for qi in range(G["p3tiles"]):
            tq = 47 + qi
            kb.dma(hto[:], HTO[:, :, qi * 128:(qi + 1) * 128], writes=[hto])
            kb.dma(sk[:], g("selkeep")[qi], writes=[sk])
            kb.dma(sa[:], g("seladd")[qi], writes=[sa])
            kb.dma(sv[:], g("selv")[qi], writes=[sv])
            pg = bank()
            for kc in range(8):
                mm(pg, pg[0:24, 0:128], WG, WG[:, kc, :], hto, hto[:, kc, :], start=(kc == 0), stop=(kc == 7))
            op("act", lambda e: e.activation(out=qsq[0:24, 0:128], in_=pg[0:24, 0:128], func=AF.Tanh, scale=0.5), reads=[pg], writes=[qsq])
            op("dve", lambda e: e.tensor_scalar(out=gT[:], in0=qsq[0:24, 0:128], scalar1=0.5, scalar2=0.5, op0=ALU.mult, op1=ALU.add), reads=[qsq], writes=[gT])
            for gi in range(2):
                GB = GBs[gbi[0] % 2]
                gbi[0] += 1
                pq = bank()
                for r in range(4):
                    c0 = (gi * 4 + r) * 64
                    for kc in range(8):
                        mm(pq, pq[0:64, r * 128:(r + 1) * 128], WQ, WQ[:, kc, c0:c0 + 64], hto, hto[:, kc, :], start=(kc == 0), stop=(kc == 7))
                op("act", lambda e: e.activation(out=qsq[0:64, :], in_=pq[0:64, :], func=AF.Square), reads=[pq], writes=[qsq])
                pq2 = bank()
                mm(pq2, pq2[0:64, :], ones_f, ones_f[0:64, 0:64], qsq, qsq[0:64, :])
                op("dve", lambda e: e.tensor_scalar(out=qrs[0:64, :], in0=pq2[0:64, :], scalar1=1.0 / 64, scalar2=1e-6, op0=ALU.mult, op1=ALU.add), reads=[pq2], writes=[qrs])
                op("act", lambda e: e.activation(out=qrs[0:64, :], in_=qrs[0:64, :], func=AF.Ln), reads=[qrs], writes=[qrs])
                op("act", lambda e: e.activation(out=qrs[0:64, :], in_=qrs[0:64, :], func=AF.Exp, scale=-0.5), reads=[qrs], writes=[qrs])
                op("dve", lambda e: e.scalar_tensor_tensor(out=QNf[0:64, :], in0=pq[0:64, :], scalar=QNG[0:64, 0:1], in1=qrs[0:64, :], op0=ALU.mult, op1=ALU.mult),
                   reads=[pq, QNG, qrs], writes=[QNf])
                op("pool", lambda e: e.tensor_copy(out=QNb[0:64, :], in_=QNf[0:64, :]), reads=[QNf], writes=[QNb])
                for br in range(3):
                    pgb = bank()
                    for r in range(4):
                        j = (gi * 4 + r) * 3 + br
                        mm(pgb, pgb[0:64, r * 128:(r + 1) * 128], GSEL, GSEL[:, j * 64:(j + 1) * 64], gT, gT[:])
                    op("act", lambda e: e.activation(out=GB[:, br, :], in_=pgb[0:64, :], func=AF.Copy), reads=[pgb], writes=[GB])

                if P3LVL < 2:
                    continue
                cts = [ct for ct in range(4) if tq - 16 * ct >= 0]
                for ii, ct in enumerate(cts):
                    m = tq - 16 * ct
                    ps_ = bank()
                    mm(ps_, ps_[:], KCf[gi], KCf[gi][0:64, ct * 128:(ct + 1) * 128], QNf, QNf[0:64, :])
                    if m <= 17:
                        bt_ = bmt[ii % 2]
                        kb.dma(bt_[:], g("bmc")[m, gi], writes=[bt_])
                    else:
                        bt_ = B31[gi]
                    op("dve", lambda e: e.tensor_tensor(out=PC[ct][:], in0=ps_[:], in1=bt_[:], op=ALU.add), reads=[ps_, bt_], writes=[PC[ct]])
                    op("act", lambda e: e.activation(out=PC[ct][:], in_=PC[ct][:], func=AF.Exp, bias=CVO[:, ct:ct + 1]), reads=[PC[ct], CVO], writes=[PC[ct]])
                for ii, ct in enumerate(cts):
                    mm(ACC0, ACC0[:], VCf[gi], VCf[gi][:, ct, :], PC[ct], PC[ct][:], start=(ii == 0), stop=(ii == len(cts) - 1))
                if P3SUB < 3:
                    continue
                finish_branch(0, True)
                if P3LVL < 3:
                    continue
                pis = [bank(), bank()]
                for r in range(4):
                    pb_ = pis[r // 2]
                    c0_ = (r % 2) * 132
                    for ii, ct in enumerate(cts):
                        mm(pb_, pb_[:, c0_:c0_ + 129], PC[ct], PC[ct][:, r * 128:(r + 1) * 128], OVL, OVL[:, ct, 0:129],
                           start=(ii == 0), stop=(ii == len(cts) - 1))
                for r in range(4):
                    pb_ = pis[r // 2]
                    c0_ = (r % 2) * 132
                    op("dve", lambda e: e.tensor_scalar(out=rdq[:, r:r + 1], in0=pb_[:, c0_ + 128:c0_ + 129], scalar1=1e-30, scalar2=None, op0=ALU.max),
                       reads=[pb_], writes=[rdq])
                op("dve", lambda e: e.reciprocal(out=rdq[:], in_=rdq[:]), reads=[rdq], writes=[rdq])
                op("dve", lambda e: e.tensor_scalar(out=sc2[:], in0=pis[0][:, 0:128], scalar1=rdq[:, 0:1], scalar2=None, op0=ALU.mult), reads=[pis[0], rdq], writes=[sc2])
                for r in range(1, 4):
                    pb_ = pis[r // 2]
                    c0_ = (r % 2) * 132
                    op("dve", lambda e: e.scalar_tensor_tensor(out=sc2[:], in0=pb_[:, c0_:c0_ + 128], scalar=rdq[:, r:r + 1], in1=sc2[:],
                                                               op0=ALU.mult, op1=ALU.add), reads=[pb_, rdq, sc2], writes=[sc2])
                op("dve", lambda e: e.tensor_tensor(out=sc[:], in0=sc2[:], in1=sk[:], op=ALU.mult), reads=[sc2, sk], writes=[sc])
                if dbg_o:
                    kb.dma(dbg_o["imp"][qi, gi], sc[:], reads=[sc])
                op("dve", lambda e: e.tensor_tensor(out=sc[:], in0=sc[:], in1=sa[:], op=ALU.add), reads=[sc, sa], writes=[sc])
                op("dve", lambda e: e.max(out=m8a[:], in_=sc[:]), reads=[sc], writes=[m8a])
                op("dve", lambda e: e.match_replace(out=sc2[:], in_to_replace=m8a[:], in_values=sc[:], imm_value=-3.0e38), reads=[sc, m8a], writes=[sc2])
                op("dve", lambda e: e.max(out=m8b[:], in_=sc2[:]), reads=[sc2], writes=[m8b])
                op("dve", lambda e: e.tensor_scalar(out=sc2[:], in0=sc[:], scalar1=m8b[:, 7:8], scalar2=None, op0=ALU.is_ge), reads=[sc, m8b], writes=[sc2])
                op("dve", lambda e: e.tensor_tensor(out=smb[:], in0=sc2[:], in1=sv[:], op=ALU.mult), reads=[sc2, sv], writes=[smb])

                if P3LVL < 4:
                    continue
                def run_steps(steps):
                    pend = []

                    def stage_c(st_, P):
                        mm(ACC0, ACC0[:], st_["vb"], st_["vap"], P, P[:], start=st_["first"], stop=st_["last"])
                    for st_ in steps:
                        ps_ = bank()
                        mm(ps_, ps_[:], st_["kb"], st_["kap"], QNb, QNb[0:64, :])
                        me = None
                        if st_["me"]:
                            me = MES[mi[0] % 3]
                            mi[0] += 1
                            mm(me, me[:, 0:128], BIGE, BIGE[:, st_["j"] * 128:(st_["j"] + 1) * 128], mT, mT[:])
                        E = Eb[ei[0] % 4]
                        P = Pb[ei[0] % 4]
                        ei[0] += 1
                        mk = st_["mask"]
                        if me is None and mk is None:
                            op("act", lambda e: e.activation(out=P[:], in_=ps_[:], func=AF.Exp), reads=[ps_], writes=[P])
                        else:
                            op("act", lambda e: e.activation(out=E[:], in_=ps_[:], func=AF.Exp), reads=[ps_], writes=[E])
                            if me is not None:
                                op("dve", lambda e: e.tensor_tensor(out=P[:].rearrange("p (h q) -> p h q", h=4), in0=E[:].rearrange("p (h q) -> p h q", h=4),
                                                                    in1=me[:, 0:128].unsqueeze(1).to_broadcast([128, 4, 128]), op=ALU.mult), reads=[E, me], writes=[P])
                                if mk is not None:
                                    op("pool", lambda e: e.tensor_tensor(out=P[:], in0=P[:], in1=mk[:], op=ALU.mult), reads=[P, mk], writes=[P])
                            elif mk is LSb:
                                op("pool", lambda e: e.tensor_tensor(out=P[:].rearrange("p (h q) -> p h q", h=4), in0=E[:].rearrange("p (h q) -> p h q", h=4),
                                                                     in1=LSb[:].unsqueeze(1).to_broadcast([128, 4, 128]), op=ALU.mult), reads=[E, LSb], writes=[P])
                            else:
                                op("pool", lambda e: e.tensor_tensor(out=P[:], in0=E[:], in1=mk[:], op=ALU.mult), reads=[E, mk], writes=[P])
                        pend.append((st_, P))
                        if len(pend) > 2:
                            stage_c(*pend.pop(0))
                        if st_["me"] and epg[0] is not None and st_["j"] % 2 == 1:
                            next(epg[0], None)
                    while pend:
                        stage_c(*pend.pop(0))

                steps = []
                for o in range(4, -1, -1):
                    j = tq - o
                    jw = j - 42
                    steps.append(dict(j=j, kb=KWs[gi], kap=KWs[gi][0:64, jw * 128:(jw + 1) * 128], me=False,
                                      mask=(LSb if o == 4 else EMD[gi] if o == 0 else EMS[gi] if o == 1 else None),
                                      vb=VW1[gi], vap=VW1[gi][:, jw, :], first=(o == 4), last=(o == 0)))
                run_steps(steps)
                finish_branch(2, False)
                op("pe", lambda e: e.transpose(out=PT[:, 0:128], in_=smb[:], identity=idb[:]), reads=[smb, idb], writes=[PT])
                op("act", lambda e: e.activation(out=mT[:], in_=PT[:, 0:128], func=AF.Copy), reads=[PT], writes=[mT])
                steps = []
                for j in range(tq + 1):
                    steps.append(dict(j=j, kb=KSs[gi], kap=KSs[gi][0:64, j * 128:(j + 1) * 128], me=True,
                                      mask=(EMD[gi] if j == tq else EMS[gi] if j == tq - 1 else None),
                                      vb=VS1[gi], vap=VS1[gi][:, j, :], first=(j == 0), last=(j == tq)))
                run_steps(steps)
                finish_branch(1, False)
                if epg[0] is not None:
                    for _ in epg[0]:
                        pass
                epg[0] = epilogue(GB, accs[acs[0] % 2], qi, gi)
                acs[0] += 1
        if epg[0] is not None:
            for _ in epg[0]:
                pass
        if dbg_o:
            kb.barrier()
            kb.dma(dbg_o["ont"], ONT)


def phase4(nc, kb, st0, G):
    op = kb.op
    mm = G["mm"]
    bank = G["bank"]
    PT = G["PT"]
    idb = G["idb"]
    xp, w_out, g2n, ffn_up, ffn_down, cw, cb, hmask = (G[k] for k in ("xp", "w_out", "g2n", "ffn_up", "ffn_down", "cw", "cb", "hmask"))
    ORT, ONT, y = G["ORT"], G["ONT"], G["y"]
    X1S, H2S = G["X1S"], G["H2S"]
    ntl = G["p4tiles"]
    with ExitStack() as st:
        def sb(shape, dt):
            return kb.sb(shape, dt, st)
        FU = sb([128, 8, 5632], BF16)
        FD = sb([128, 22, 1024], BF16)
        HIST = sb([128, 44, 2], F32)
        CW = sb([128, 44, 3], F32)
        CB = sb([128, 44], F32)
        HM = sb([128, 1], F32)
        kb.dma(CW[:], cw, writes=[CW])
        kb.dma(CB[:], cb, writes=[CB])
        kb.dma(HM[:], hmask, writes=[HM])
        op("pool", lambda e: e.memset(HIST[:], 0.0), writes=[HIST])
        with ExitStack() as sta:
            def sba(shape, dt):
                return kb.sb(shape, dt, sta)
            WO = sba([128, 8, 1024], BF16)
            G2B = sba([128, 1024], F32)
            kb.dma(G2B[:], g2n.partition_broadcast(128), writes=[G2B])
            stg = [sba([128, 1408], F32) for _ in range(3)]
            n = 0
            engs = ("dve", "pool", "act")

            def cast(dst_b, dst_ap, src_ap, w):
                nonlocal n
                s = stg[n % 3]
                eng = engs[n % 3]
                n += 1
                kb.dma(s[:, 0:w], src_ap, writes=[s])
                if eng == "act":
                    op("act", lambda e: e.activation(out=dst_ap, in_=s[:, 0:w], func=AF.Copy), reads=[s], writes=[dst_b])
                else:
                    op(eng, lambda e: e.tensor_copy(out=dst_ap, in_=s[:, 0:w]), reads=[s], writes=[dst_b])
            for kc in range(8):
                cast(WO, WO[:, kc, :], w_out[kc * 128:(kc + 1) * 128, :], 1024)
            xt = sba([128, 1024], F32)
            x1 = sba([128, 1024], F32)
            xn2 = sba([128, 1024], BF16)
            ss = sba([128, 1], F32)
            h2T = sba([128, 8, 128], BF16)
            oT = sba([128, 8, 128], BF16)
            fu_jobs = [(kc, q) for kc in range(8) for q in range(4)]
            fd_jobs = list(range(22))
            for ti in range(ntl):
                tt = 47 + ti
                kb.dma(xt[:], xp[tt * 128:(tt + 1) * 128, :], writes=[xt])
                kb.dma(oT[:, 0:4, :], ONT[:, ti * 128:(ti + 1) * 128].rearrange("(c p) t -> p c t", p=128), writes=[oT])
                kb.dma(oT[:, 4:8, :], ORT[:, ti * 128:(ti + 1) * 128].rearrange("(c p) t -> p c t", p=128), writes=[oT])
                for hf in range(2):
                    pb = bank()
                    for kc in range(8):
                        mm(pb, pb[:], oT, oT[:, kc, :], WO, WO[:, kc, hf * 512:(hf + 1) * 512], start=(kc == 0), stop=(kc == 7))
                    op("dve", lambda e: e.tensor_tensor(out=x1[:, hf * 512:(hf + 1) * 512], in0=pb[:], in1=xt[:, hf * 512:(hf + 1) * 512], op=ALU.add),
                       reads=[pb, xt], writes=[x1])
                kb.dma(X1S[ti * 128:(ti + 1) * 128, :], x1[:], reads=[x1], eng="pool")
                op("act", lambda e: e.activation(out=xn2[:], in_=x1[:], func=AF.Square, accum_out=ss[:]), reads=[x1], writes=[xn2, ss])
                op("dve", lambda e: e.tensor_scalar(out=ss[:], in0=ss[:], scalar1=1.0 / 1024, scalar2=1e-6, op0=ALU.mult, op1=ALU.add), reads=[ss], writes=[ss])
                op("act", lambda e: e.activation(out=ss[:], in_=ss[:], func=AF.Sqrt), reads=[ss], writes=[ss])
                op("dve", lambda e: e.reciprocal(out=ss[:], in_=ss[:]), reads=[ss], writes=[ss])
                op("dve", lambda e: e.scalar_tensor_tensor(out=xn2[:], in0=x1[:], scalar=ss[:, 0:1], in1=G2B[:], op0=ALU.mult, op1=ALU.mult),
                   reads=[x1, ss, G2B], writes=[xn2])
                for kc in range(8):
                    op("pe", lambda e: e.transpose(out=PT[:, kc * 128:(kc + 1) * 128], in_=xn2[:, kc * 128:(kc + 1) * 128], identity=idb[:]),
                       reads=[xn2, idb], writes=[PT])
                op("act", lambda e: e.activation(out=h2T[:], in_=PT[:].rearrange("p (k t) -> p k t", k=8), func=AF.Copy), reads=[PT], writes=[h2T])
                kb.dma(H2S[:, :, ti * 128:(ti + 1) * 128], h2T[:], reads=[h2T], eng="pool")
                for _ in range(4):
                    if fu_jobs:
                        kc, q = fu_jobs.pop(0)
                        cast(FU, FU[:, kc, q * 1408:(q + 1) * 1408], ffn_up[kc * 128:(kc + 1) * 128, q * 1408:(q + 1) * 1408], 1408)
                for _ in range(2):
                    if fd_jobs:
                        j = fd_jobs.pop(0)
                        cast(FD, FD[:, j, :], ffn_down[j * 128:(j + 1) * 128, :], 1024)
            while fu_jobs:
                kc, q = fu_jobs.pop(0)
                cast(FU, FU[:, kc, q * 1408:(q + 1) * 1408], ffn_up[kc * 128:(kc + 1) * 128, q * 1408:(q + 1) * 1408], 1408)
            while fd_jobs:
                j = fd_jobs.pop(0)
                cast(FD, FD[:, j, :], ffn_down[j * 128:(j + 1) * 128, :], 1024)
            kb.barrier()

        h2b = sb([128, 8, 512], BF16)
        ub = [sb([128, 514], F32) for _ in range(4)]
        uu = [sb([128, 512], F32) for _ in range(4)]
        sgt = [sb([128, 512], F32) for _ in range(2)]
        actT = sb([128, 22, 512], BF16)
        x1b = sb([128, 1024], F32)
        yt = sb([128, 1024], F32)
        blocks = [(0, 1)] + [(1 + 4 * k, 4) for k in range(4)]
        for (ti0, nt_) in blocks:
            if ti0 >= ntl:
                break
            nt_ = min(nt_, ntl - ti0)
            W = 128 * nt_
            halo = ti0 == 0
            kb.dma(h2b[:, :, 0:W], H2S[:, :, ti0 * 128:ti0 * 128 + W], writes=[h2b])
            for jj in range(22):
                us = []
                for k2, j in enumerate((jj, jj + 22)):
                    pb = bank()
                    for kc in range(8):
                        mm(pb, pb[:, 0:W], FU, FU[:, kc, j * 128:(j + 1) * 128], h2b, h2b[:, kc, 0:W], start=(kc == 0), stop=(kc == 7))
                    U = ub[(2 * jj + k2) % 4]
                    op("pool", lambda e: e.tensor_copy(out=U[:, 0:2], in_=HIST[:, j, :]), reads=[HIST], writes=[U])
                    op("act", lambda e: e.activation(out=U[:, 2:2 + W], in_=pb[:, 0:W], func=AF.Copy), reads=[pb], writes=[U])
                    op("pool", lambda e: e.tensor_copy(out=HIST[:, j, :], in_=U[:, W:W + 2]), reads=[U], writes=[HIST])
                    if halo:
                        continue
                    u = uu[(2 * jj + k2) % 4]
                    op("act", lambda e: e.activation(out=u[:, 0:W], in_=U[:, 2:2 + W], func=AF.Identity, bias=CB[:, j:j + 1], scale=CW[:, j, 2:3]),
                       reads=[U, CB, CW], writes=[u])
                    op("dve", lambda e: e.scalar_tensor_tensor(out=u[:, 0:W], in0=U[:, 1:1 + W], scalar=CW[:, j, 1:2], in1=u[:, 0:W], op0=ALU.mult, op1=ALU.add),
                       reads=[U, CW, u], writes=[u])
                    op("dve", lambda e: e.scalar_tensor_tensor(out=u[:, 0:W], in0=U[:, 0:W], scalar=CW[:, j, 0:1], in1=u[:, 0:W], op0=ALU.mult, op1=ALU.add),
                       reads=[U, CW, u], writes=[u])
                    us.append(u)
                if halo:
                    continue
                sgb = sgt[jj % 2]
                op("act", lambda e: e.activation(out=sgb[:, 0:W], in_=us[1][:, 0:W], func=AF.Silu), reads=[us[1]], writes=[sgb])
                op("pool", lambda e: e.tensor_tensor(out=actT[:, jj, 0:W], in0=sgb[:, 0:W], in1=us[0][:, 0:W], op=ALU.mult), reads=[sgb, us[0]], writes=[actT])
            if halo:
                op("dve", lambda e: e.tensor_scalar(out=HIST[:], in0=HIST[:], scalar1=HM[:, 0:1], scalar2=None, op0=ALU.mult),
                   reads=[HIST, HM], writes=[HIST])
                continue
            for tl in range(nt_):
                ti = ti0 + tl
                kb.dma(x1b[:], X1S[ti * 128:(ti + 1) * 128, :], writes=[x1b])
                for hf in range(2):
                    pb = bank()
                    for j in range(22):
                        mm(pb, pb[:], actT, actT[:, j, tl * 128:(tl + 1) * 128], FD, FD[:, j, hf * 512:(hf + 1) * 512], start=(j == 0), stop=(j == 21))
                    op("dve", lambda e: e.tensor_tensor(out=yt[:, hf * 512:(hf + 1) * 512], in0=pb[:], in1=x1b[:, hf * 512:(hf + 1) * 512], op=ALU.add),
                       reads=[pb, x1b], writes=[yt])
                kb.dma(y[(ti - 1) * 128:ti * 128, :], yt[:], reads=[yt], eng="pool")


def shared_inputs(inp):
    f = np.float32
    w_in = np.asarray(inp["w_in"][0], f)
    kc, vc = w_in[:, 512:640], w_in[:, 640:768]
    ksl, vsl = w_in[:, 768:896], w_in[:, 896:1024]
    kwn, vwn = w_in[:, 1024:1152], w_in[:, 1152:1280]
    d = {}
    d["g1"] = np.asarray(inp["norm1_g"], f).reshape(1, 1024)
    d["w_rw"] = np.ascontiguousarray(w_in[:, 1304:3096])
    d["mu"] = np.asarray(inp["rwkv_mu"], f).reshape(1, 1792)
    d["w_tm"] = np.ascontiguousarray(np.concatenate([vsl, vwn, ksl, kwn, kc, vc], 1))
    d["rwp"] = np.stack([np.asarray(inp[k], f).reshape(512) for k in
                         ("w0", "a0", "k_k", "k_a", "r_k", "ln_x_w", "ln_x_b")], 0)
    d["w2"] = np.asarray(inp["w2"][0], f)
    d["a2"] = np.asarray(inp["a2"][0], f)
    d["g2"] = np.asarray(inp["g2"][0], f)
    kg = np.asarray(inp["k_norm_g"][0], f)
    d["kngr"] = np.concatenate([kg[1], kg[1], kg[2], kg[2]]).reshape(1, 256)
    r = np.arange(128)
    d["tri"] = np.stack([(r[:, None] <= r[None, :]), (r[:, None] < r[None, :]), (r[:, None] > r[None, :])], 0).astype(f)
    d["idf"] = np.eye(128, dtype=f)
    d["bo"] = ((r[:, None] // 64) == (r[None, :] // 64)).astype(f)
    import ml_dtypes
    bf = ml_dtypes.bfloat16
    d["w_q"] = np.ascontiguousarray(w_in[:, 0:512])
    d["w_g"] = np.ascontiguousarray(w_in[:, 1280:1304])
    d["qng"] = np.tile(np.asarray(inp["q_norm_g"][0], f), 2).reshape(128, 1)
    d["kcg"] = np.tile(kg[0], 2).reshape(128, 1)
    w1 = np.asarray(inp["cmp_w1"][0], f).reshape(2, 32, 64, 128).transpose(0, 2, 1, 3)
    d["cw1"] = np.ascontiguousarray(np.concatenate([w1, w1], 1))
    ps = np.asarray(inp["cmp_pos"][0], f).transpose(0, 2, 1)
    d["cpos"] = np.ascontiguousarray(np.concatenate([ps, ps], 1))
    d["cb1"] = np.ascontiguousarray(np.asarray(inp["cmp_b1"][0], f).T)
    d["cw2"] = np.asarray(inp["cmp_w2"][0], f)
    b2 = np.asarray(inp["cmp_b2"][0], f)
    d["cb2k"] = np.tile(b2[0], 2).reshape(128, 1)
    d["cb2v"] = b2[1].reshape(1, 64)
    rb = np.asarray(inp["rel_bias"], f)
    n = np.arange(0, NT + 256)
    nf = np.maximum(n, 1).astype(f)
    large = 16 + (np.log(nf / f(16)) / f(math.log(8.0)) * f(16)).astype(np.int32)
    bk = np.where(n < 16, n, np.minimum(large, 31))
    q = np.arange(128)
    bmc = np.empty((18, 2, 128, 4, 128), f)
    for m in range(18):
        dd = (-31 + 128 * m) + q[None, :] - 16 * q[:, None]
        for gi in range(2):
            for h in range(4):
                bmc[m, gi, :, h, :] = np.where(dd >= 0, rb[bk[np.maximum(dd, 0)], 4 * gi + h], f(-1e30))
    d["bmc"] = bmc.reshape(18, 2, 128, 512)
    b31 = np.empty((2, 128, 4, 128), f)
    btd = np.empty((2, 128, 4, 128), f)
    bts = np.empty((2, 128, 4, 128), f)
    dq = q[None, :] - q[:, None]
    for gi in range(2):
        for h in range(4):
            b31[gi, :, h, :] = rb[31, 4 * gi + h]
            btd[gi, :, h, :] = rb[bk[np.maximum(dq, 0)], 4 * gi + h]
            bts[gi, :, h, :] = rb[bk[128 + dq], 4 * gi + h]
    d["b31t"] = b31.reshape(2, 128, 512)
    d["bt_diag"] = btd.reshape(2, 128, 512)
    d["bt_sub"] = bts.reshape(2, 128, 512)
    c = np.arange(512)[:, None]
    nn = np.arange(128)[None, :]
    ov = ((16 * c < 64 * nn + 64) & (16 * c + 31 >= 64 * nn) & (c < 511)).astype(f)
    ov1 = np.zeros((512, 132), f)
    ov1[:, 0:128] = ov
    ov1[:511, 128] = 1.0
    d["ovl"] = np.ascontiguousarray(ov1.reshape(4, 128, 132).transpose(1, 0, 2))
    d["bige"] = ((np.arange(NT)[None, :] // 64) == np.arange(128)[:, None]).astype(bf)
    gs = np.zeros((24, 24, 64), f)
    gs[np.arange(24), np.arange(24), :] = 1.0
    d["gsel"] = gs.reshape(24, 24 * 64).astype(bf)
    d["w_out"] = np.asarray(inp["w_out"][0], f)
    d["g2n"] = np.asarray(inp["norm2_g"], f).reshape(1, 1024)
    d["ffn_up"] = np.asarray(inp["ffn_up"][0], f)
    d["ffn_down"] = np.asarray(inp["ffn_down"][0], f)
    d["cw"] = np.ascontiguousarray(np.asarray(inp["conv_w"][0], f).T.reshape(44, 128, 3).transpose(1, 0, 2))
    d["cb"] = np.ascontiguousarray(np.asarray(inp["conv_b"][0], f).reshape(44, 128).T)
    return d


def core_inputs(inp, c):
    b, i = c // 4, c % 4
    pad = NT - NOWN * (i + 1)
    xp = np.zeros((NT, 1024), np.float32)
    xp[pad:] = inp["x"][b, :NOWN * (i + 1)]
    import ml_dtypes
    f = np.float32
    nb0 = pad // 64
    t = (47 + np.arange(17))[:, None, None] * 128 + np.arange(128)[None, :, None]
    n = np.arange(128)[None, None, :]
    valid = (n >= nb0) & (64 * n <= t) & (t >= pad)
    cur = t // 64
    forced = valid & ((n == nb0) | (n == cur) | (n == cur - 1))
    keep = (valid & ~forced).astype(f)
    add = np.where(forced, f(1e9), f(0.0)) + np.where(valid, f(0.0), f(-1e30))
    cc = np.arange(128)[:, None] + 128 * np.arange(4)[None, :]
    cvoid = np.where(16 * cc < pad, f(-1e30), f(0.0)).astype(f)
    onesv = np.zeros((128, 64, 64), f)
    onesv[:, pad // 128:, :] = 1.0
    return {"xp": xp, "hmask": np.full((128, 1), 0.0 if i == 0 else 1.0, f),
            "selkeep": keep, "seladd": add.astype(f), "selv": valid.astype(f), "cvoid": cvoid,
            "onesv": onesv.astype(ml_dtypes.bfloat16)}


def kernel(**inputs):
    inp = {k: np.asarray(v) for k, v in inputs.items()}
    nc = build()
    sh = shared_inputs(inp)
    in_maps = []
    for c in range(8):
        m = dict(sh)
        m.update(core_inputs(inp, c))
        in_maps.append(m)
    res = run_bass_kernel_spmd(nc, in_maps, core_ids=list(range(8)))
    out = np.empty((2, 8192, 1024), np.float32)
    for c in range(8):
        b, i = c // 4, c % 4
        out[b, NOWN * i:NOWN * (i + 1)] = res.results[c]["y"]
    return out
```
